# Optimizing a Trainium2 kernel written in Bass

```python
import math
import jax, jax.numpy as jnp
from jax import lax
import numpy as np

D_MODEL = 1024
BATCH = 8
SEQ = 4096
DEPTH = 4

N_MIXERS = 3
N_MLA_LAYERS = (DEPTH + 2) // 3
N_RET_LAYERS = (DEPTH + 1) // 3
N_GLA_LAYERS = DEPTH // 3

DEEPNORM_ALPHA = (2.0 * DEPTH) ** 0.25
DEEPNORM_BETA = (8.0 * DEPTH) ** -0.25
LN_EPS = 1e-5
RMS_EPS = 1e-6

FFN_HIDDEN = -(-8 * D_MODEL // (3 * 256)) * 256

ROPE_BASE = 10000.0

MLA_HEADS = 8
MLA_NOPE = 128
MLA_ROPE = 64
MLA_V = 128
MLA_Q_RANK = 512
MLA_KV_RANK = 256
MLA_Q_BLOCK = 128

RET_HEADS = 4
RET_QK_DIM = D_MODEL // RET_HEADS
RET_V_DIM = 2 * RET_QK_DIM
RET_CHUNK = 128

GLA_HEADS = 4
GLA_KEY_DIM = (D_MODEL // 2) // GLA_HEADS
GLA_V_DIM = D_MODEL // GLA_HEADS
GLA_GATE_RANK = 16
GLA_TAU = 16.0
GLA_CHUNK = 64

kernel_name = "hybrid_mla_retention_gla_deepnorm_encoder"


def layer_norm(x, g, b):
    xf = x.astype(jnp.float32)
    mu = jnp.mean(xf, axis=-1, keepdims=True)
    var = jnp.mean(jnp.square(xf - mu), axis=-1, keepdims=True)
    return ((xf - mu) * lax.rsqrt(var + LN_EPS) * g.astype(jnp.float32) + b.astype(jnp.float32)).astype(x.dtype)


def rms_norm(x, g):
    xf = x.astype(jnp.float32)
    ms = jnp.mean(jnp.square(xf), axis=-1, keepdims=True)
    return (xf * lax.rsqrt(ms + RMS_EPS) * g.astype(jnp.float32)).astype(x.dtype)


def apply_rotary(x, positions):
    d = x.shape[-1]
    inv_freq = ROPE_BASE ** (-jnp.arange(0, d, 2, dtype=jnp.float32) / d)
    ang = positions.astype(jnp.float32)[..., None] * inv_freq
    cos = jnp.cos(ang)[:, :, None, :]
    sin = jnp.sin(ang)[:, :, None, :]
    xf = x.astype(jnp.float32)
    x1, x2 = xf[..., : d // 2], xf[..., d // 2:]
    return jnp.concatenate([x1 * cos - x2 * sin, x1 * sin + x2 * cos], axis=-1).astype(x.dtype)


def flip_seq(a):
    return jnp.flip(a, axis=1)


def mla_mixer(x, positions, w_down, q_norm, w_uq, kv_norm, w_ukv, w_o):
    B, S, _ = x.shape
    H = MLA_HEADS
    down = x @ w_down
    c_q = down[..., :MLA_Q_RANK]
    c_kv = down[..., MLA_Q_RANK:MLA_Q_RANK + MLA_KV_RANK]
    k_rope = down[..., MLA_Q_RANK + MLA_KV_RANK:]
    q = (rms_norm(c_q, q_norm) @ w_uq).reshape(B, S, H, MLA_NOPE + MLA_ROPE)
    q_nope = q[..., :MLA_NOPE]
    q_rope = apply_rotary(q[..., MLA_NOPE:], positions)
    kv = (rms_norm(c_kv, kv_norm) @ w_ukv).reshape(B, S, H, MLA_NOPE + MLA_V)
    k_nope = kv[..., :MLA_NOPE]
    v = kv[..., MLA_NOPE:]
    k_rope = apply_rotary(k_rope[:, :, None, :], positions)[:, :, 0, :]
    scale = (MLA_NOPE + MLA_ROPE) ** -0.5
    nb = S // MLA_Q_BLOCK
    qn_blocks = jnp.moveaxis(q_nope.reshape(B, nb, MLA_Q_BLOCK, H, MLA_NOPE), 1, 0)
    qr_blocks = jnp.moveaxis(q_rope.reshape(B, nb, MLA_Q_BLOCK, H, MLA_ROPE), 1, 0)

    def attend(blk):
        qn_b, qr_b = blk
        s = (jnp.einsum('bqhd,bkhd->bhqk', qn_b, k_nope)
             + jnp.einsum('bqhr,bkr->bhqk', qr_b, k_rope)).astype(jnp.float32) * scale
        p = jax.nn.softmax(s, axis=-1)
        return jnp.einsum('bhqk,bkhd->bqhd', p.astype(v.dtype), v)

    o = lax.map(attend, (qn_blocks, qr_blocks))
    o = jnp.moveaxis(o, 0, 1).reshape(B, S, H * MLA_V)
    return o @ w_o


def retention_chunked(q, k, v, log_gamma, strict):
    B, S, H, DK = q.shape
    DV = v.shape[-1]
    C = RET_CHUNK
    N = S // C
    qc = q.reshape(B, N, C, H, DK)
    kc = k.reshape(B, N, C, H, DK)
    vc = v.reshape(B, N, C, H, DV)
    lg = log_gamma.astype(jnp.float32)
    idx = jnp.arange(C, dtype=jnp.float32)
    diff = idx[:, None] - idx[None, :]
    mask = (diff > 0) if strict else (diff >= 0)
    dmat = jnp.where(mask[None], jnp.exp(jnp.maximum(diff, 0.0)[None] * lg[:, None, None]), 0.0)
    scores = jnp.einsum('bnihd,bnjhd->bnhij', qc, kc) * dmat[None, None]
    intra = jnp.einsum('bnhij,bnjhe->bnihe', scores, vc)
    q_decay = jnp.exp((idx[:, None] + 1.0) * lg[None])
    k_decay = jnp.exp((C - 1.0 - idx)[:, None] * lg[None])
    chunk_decay = jnp.exp(C * lg)

    def step(state, xs):
        q_n, k_n, v_n = xs
        cross = jnp.einsum('bihd,bhde->bihe', q_n, state) * q_decay[None, :, :, None]
        state = (state * chunk_decay[None, :, None, None]
                 + jnp.einsum('bjhd,bjhe->bhde', k_n * k_decay[None, :, :, None], v_n))
        return state, cross

    state0 = jnp.zeros((B, H, DK, DV), jnp.float32)
    _, cross = lax.scan(step, state0, (jnp.moveaxis(qc, 1, 0), jnp.moveaxis(kc, 1, 0), jnp.moveaxis(vc, 1, 0)))
    cross = jnp.moveaxis(cross, 0, 1)
    return (intra + cross).reshape(B, S, H, DV)


def retention_mixer(x, positions, w_in, decay_logit, gn_w, gn_b, w_o):
    B, S, _ = x.shape
    H, DK, DV = RET_HEADS, RET_QK_DIM, RET_V_DIM
    proj = x @ w_in
    q = proj[..., : H * DK].reshape(B, S, H, DK)
    k = proj[..., H * DK: 2 * H * DK].reshape(B, S, H, DK)
    v = proj[..., 2 * H * DK: 2 * H * DK + H * DV].reshape(B, S, H, DV).astype(jnp.float32)
    g = proj[..., 2 * H * DK + H * DV:]
    q = apply_rotary(q, positions).astype(jnp.float32)
    k = apply_rotary(k, positions).astype(jnp.float32) * (DK ** -0.5)
    log_gamma = jax.nn.log_sigmoid(decay_logit.astype(jnp.float32))
    y = (retention_chunked(q, k, v, log_gamma[0], False)
         + flip_seq(retention_chunked(flip_seq(q), flip_seq(k), flip_seq(v), log_gamma[1], True)))
    mu = jnp.mean(y, axis=-1, keepdims=True)
    var = jnp.mean(jnp.square(y - mu), axis=-1, keepdims=True)
    y = (y - mu) * lax.rsqrt(var + LN_EPS)
    y = y.reshape(B, S, H * DV) * gn_w.astype(jnp.float32) + gn_b.astype(jnp.float32)
    return (jax.nn.silu(g) * y.astype(x.dtype)) @ w_o


def gla_chunked(q, k, v, log_alpha, strict):
    B, S, H, DK = q.shape
    DV = v.shape[-1]
    C = GLA_CHUNK
    N = S // C
    qc = q.reshape(B, N, C, H, DK)
    kc = k.reshape(B, N, C, H, DK)
    vc = v.reshape(B, N, C, H, DV)
    b = jnp.cumsum(log_alpha.reshape(B, N, C, H, DK), axis=2)
    q_in = qc * jnp.exp(b)
    k_in = kc * jnp.exp(-b)
    idx = jnp.arange(C)
    mask = (idx[:, None] > idx[None, :]) if strict else (idx[:, None] >= idx[None, :])
    scores = jnp.where(mask, jnp.einsum('bnihd,bnjhd->bnhij', q_in, k_in), 0.0)
    intra = jnp.einsum('bnhij,bnjhe->bnihe', scores, vc)
    b_last = b[:, :, -1]
    k_state = kc * jnp.exp(b_last[:, :, None] - b)

    def step(state, xs):
        q_n, k_n, v_n, bl_n = xs
        cross = jnp.einsum('bihd,bhde->bihe', q_n, state)
        state = state * jnp.exp(bl_n)[..., None] + jnp.einsum('bjhd,bjhe->bhde', k_n, v_n)
        return state, cross

    state0 = jnp.zeros((B, H, DK, DV), jnp.float32)
    _, cross = lax.scan(step, state0, (jnp.moveaxis(q_in, 1, 0), jnp.moveaxis(k_state, 1, 0),
                                       jnp.moveaxis(vc, 1, 0), jnp.moveaxis(b_last, 1, 0)))
    cross = jnp.moveaxis(cross, 0, 1)
    return (intra + cross).reshape(B, S, H, DV)


def gla_mixer(x, w_in, w_a1, w_a2, b_a, norm_g, w_o):
    B, S, _ = x.shape
    H, DK, DV = GLA_HEADS, GLA_KEY_DIM, GLA_V_DIM
    hk, hv = H * DK, H * DV
    proj = x @ w_in
    q = proj[..., :hk].reshape(B, S, H, DK).astype(jnp.float32) * (DK ** -0.5)
    k = proj[..., hk:2 * hk].reshape(B, S, H, DK).astype(jnp.float32)
    v = proj[..., 2 * hk:2 * hk + hv].reshape(B, S, H, DV).astype(jnp.float32)
    r = proj[..., 2 * hk + hv:]

    def log_gate(z):
        logits = ((x @ w_a1[z]) @ w_a2[z] + b_a[z]).astype(jnp.float32)
        return (jax.nn.log_sigmoid(logits) / GLA_TAU).reshape(B, S, H, DK)

    y = (gla_chunked(q, k, v, log_gate(0), False)
         + flip_seq(gla_chunked(flip_seq(q), flip_seq(k), flip_seq(v), flip_seq(log_gate(1)), True)))
    y = rms_norm(y, norm_g).reshape(B, S, hv)
    return (jax.nn.silu(r) * y.astype(x.dtype)) @ w_o


def swiglu_ffn(x, w_in, w_out):
    gate, up = jnp.split(x @ w_in, 2, axis=-1)
    return (jax.nn.silu(gate) * up) @ w_out


def setup_inputs(seed: int = 0) -> dict:
    key = jax.random.key(seed)
    keys = iter(jax.random.split(key, 40))
    f32 = jnp.float32

    def w(shape, fan_in, gain=1.0):
        return jax.random.normal(next(keys), shape, f32) * (gain * fan_in ** -0.5)

    def gain_(shape):
        return 1.0 + 0.02 * jax.random.normal(next(keys), shape, f32)

    def bias_(shape):
        return 0.02 * jax.random.normal(next(keys), shape, f32)

    D, F = D_MODEL, FFN_HIDDEN
    nA, nB, nC = N_MLA_LAYERS, N_RET_LAYERS, N_GLA_LAYERS
    beta = DEEPNORM_BETA

    x = jax.random.normal(next(keys), (BATCH, SEQ, D), f32)
    offsets = jax.random.randint(next(keys), (BATCH, 1), 0, SEQ, dtype=jnp.int32)
    positions = (jnp.arange(SEQ, dtype=jnp.int32)[None, :] + offsets).astype(jnp.int32)

    ln_g = gain_((DEPTH, 2, D))
    ln_b = bias_((DEPTH, 2, D))
    ffn_w_in = w((DEPTH, D, 2 * F), D, beta)
    ffn_w_out = w((DEPTH, F, D), F, beta)

    H = MLA_HEADS
    mla_w_down = w((nA, D, MLA_Q_RANK + MLA_KV_RANK + MLA_ROPE), D)
    mla_q_norm = gain_((nA, MLA_Q_RANK))
    mla_w_uq = w((nA, MLA_Q_RANK, H * (MLA_NOPE + MLA_ROPE)), MLA_Q_RANK)
    mla_kv_norm = gain_((nA, MLA_KV_RANK))
    mla_w_ukv = jnp.concatenate([w((nA, MLA_KV_RANK, H, MLA_NOPE), MLA_KV_RANK),
                                 w((nA, MLA_KV_RANK, H, MLA_V), MLA_KV_RANK, beta)],
                                axis=-1).reshape(nA, MLA_KV_RANK, H * (MLA_NOPE + MLA_V))
    mla_w_o = w((nA, H * MLA_V, D), H * MLA_V, beta)

    rq = RET_HEADS * RET_QK_DIM
    rv = RET_HEADS * RET_V_DIM
    ret_w_in = jnp.concatenate([w((nB, D, 2 * rq), D), w((nB, D, rv), D, beta), w((nB, D, rv), D)], axis=-1)
    base_logit = jnp.asarray(np.log(2.0 ** (5 + np.arange(RET_HEADS)) - 1.0), dtype=f32)
    ret_decay_logit = base_logit[None, None, :] + 0.01 * jax.random.normal(next(keys), (nB, 2, RET_HEADS), f32)
    ret_gn_w = gain_((nB, rv))
    ret_gn_b = bias_((nB, rv))
    ret_w_o = w((nB, rv, D), rv, beta)

    gk = GLA_HEADS * GLA_KEY_DIM
    gv = GLA_HEADS * GLA_V_DIM
    gla_w_in = jnp.concatenate([w((nC, D, 2 * gk), D), w((nC, D, gv), D, beta), w((nC, D, gv), D)], axis=-1)
    gla_w_a1 = w((nC, 2, D, GLA_GATE_RANK), D)
    gla_w_a2 = w((nC, 2, GLA_GATE_RANK, gk), GLA_GATE_RANK)
    gla_b_a = bias_((nC, 2, gk))
    gla_norm = gain_((nC, GLA_V_DIM))
    gla_w_o = w((nC, gv, D), gv, beta)

    return {"x": x, "positions": positions, "ln_g": ln_g, "ln_b": ln_b,
            "ffn_w_in": ffn_w_in, "ffn_w_out": ffn_w_out,
            "mla_w_down": mla_w_down, "mla_q_norm": mla_q_norm, "mla_w_uq": mla_w_uq,
            "mla_kv_norm": mla_kv_norm, "mla_w_ukv": mla_w_ukv, "mla_w_o": mla_w_o,
            "ret_w_in": ret_w_in, "ret_decay_logit": ret_decay_logit, "ret_gn_w": ret_gn_w,
            "ret_gn_b": ret_gn_b, "ret_w_o": ret_w_o,
            "gla_w_in": gla_w_in, "gla_w_a1": gla_w_a1, "gla_w_a2": gla_w_a2, "gla_b_a": gla_b_a,
            "gla_norm": gla_norm, "gla_w_o": gla_w_o}


def reference(x, positions, ln_g, ln_b, ffn_w_in, ffn_w_out,
              mla_w_down, mla_q_norm, mla_w_uq, mla_kv_norm, mla_w_ukv, mla_w_o,
              ret_w_in, ret_decay_logit, ret_gn_w, ret_gn_b, ret_w_o,
              gla_w_in, gla_w_a1, gla_w_a2, gla_b_a, gla_norm, gla_w_o):
    for i in range(DEPTH):
        kind = i % N_MIXERS
        j = i // N_MIXERS
        if kind == 0:
            y = mla_mixer(x, positions, mla_w_down[j], mla_q_norm[j], mla_w_uq[j],
                          mla_kv_norm[j], mla_w_ukv[j], mla_w_o[j])
        elif kind == 1:
            y = retention_mixer(x, positions, ret_w_in[j], ret_decay_logit[j],
                                ret_gn_w[j], ret_gn_b[j], ret_w_o[j])
        else:
            y = gla_mixer(x, gla_w_in[j], gla_w_a1[j], gla_w_a2[j], gla_b_a[j],
                          gla_norm[j], gla_w_o[j])
        x = layer_norm(DEEPNORM_ALPHA * x + y, ln_g[i, 0], ln_b[i, 0])
        x = layer_norm(DEEPNORM_ALPHA * x + swiglu_ffn(x, ffn_w_in[i], ffn_w_out[i]), ln_g[i, 1], ln_b[i, 1])
    return x
```

```python
import math
from contextlib import ExitStack

import numpy as np
import ml_dtypes
import concourse.bass as bass
import concourse.mybir as mybir
from concourse.bass_utils import run_bass_kernel_spmd

F32 = mybir.dt.float32
BF16 = mybir.dt.bfloat16
I32 = mybir.dt.int32
ALU = mybir.AluOpType
AF = mybir.ActivationFunctionType

D = 1024
DEPTH = 4
ALPHA = (2.0 * DEPTH) ** 0.25
LN_EPS = 1e-5
RMS_EPS = 1e-6
FH = 2816
SEM_LIMIT = 30000


class Buf:
    __slots__ = ("name", "w", "r", "dsem", "dcnt", "dgen", "uid")
    _n = [0]

    def __init__(self, name):
        Buf._n[0] += 1
        self.uid = Buf._n[0]
        self.name = name
        self.w = {}
        self.r = {}
        self.dsem = None
        self.dcnt = 0
        self.dgen = 0


class Tl:
    def __init__(self, t, b):
        self.t = t
        self.b = b

    def __getitem__(self, k):
        return self.t[k]


class Eng:
    def __init__(self, name, h):
        self.name = name
        self.h = h
        self.sem = None
        self.gen = 0
        self.cnt = 0
        self.waited = {}


class Sched:
    def __init__(self, nc, stack):
        self.nc = nc
        self.stack = stack
        self.pe = Eng("pe", nc.tensor)
        self.act = Eng("act", nc.scalar)
        self.dve = Eng("dve", nc.vector)
        self.pool = Eng("pool", nc.gpsimd)
        self.sp = Eng("sp", nc.sync)
        self.engs = [self.pe, self.act, self.dve, self.pool, self.sp]
        self.nsem = 0
        self.dma_bufs = []
        self.sem_pool = []
        for e in self.engs:
            e.sem = self._newsem(e.name)

    def _newsem(self, name):
        self.nsem += 1
        return self.stack.enter_context(self.nc.semaphore("s_%s_%d" % (name, self.nsem)))

    @staticmethod
    def _merge(deps, d, skip=None):
        for k, (sem, val) in d.items():
            if skip is not None and k[0] == skip:
                continue
            cur = deps.get(k)
            if cur is None or cur[1] < val:
                deps[k] = (sem, val)

    def _need(self, eng, deps):
        for k, (sem, val) in deps.items():
            if eng.waited.get(k, 0) < val:
                eng.h.wait_ge(sem, val)
                eng.waited[k] = val

    def op(self, eng, fn, reads=(), writes=(), sig=True):
        deps = {}
        for b in reads:
            self._merge(deps, b.w)
        for b in writes:
            self._merge(deps, b.w, skip=eng.name)
            self._merge(deps, b.r, skip=eng.name)
        self._need(eng, deps)
        if eng.cnt >= SEM_LIMIT:
            eng.sem = self._newsem(eng.name)
            eng.gen += 1
            eng.cnt = 0
        inst = fn()
        key = (eng.name, eng.gen)
        if sig:
            eng.cnt += 1
            inst.then_inc(eng.sem, 1)
            tick = (eng.sem, eng.cnt)
        else:
            tick = (eng.sem, eng.cnt + 1)
        for b in reads:
            b.r[key] = tick
        for b in writes:
            b.w[key] = tick
        return inst

    def dma(self, eng, out, in_, reads=(), writes=(), slot=None):
        deps = {}
        for b in reads:
            self._merge(deps, b.w)
        for b in writes:
            self._merge(deps, b.w)
            self._merge(deps, b.r)
        self._need(eng, deps)
        if slot.dsem is None or slot.dcnt + 16 > SEM_LIMIT:
            if slot.dsem is None:
                self.dma_bufs.append(slot)
            if slot.dsem is None and self.sem_pool and self.sem_pool[-1][1] + 16 <= SEM_LIMIT:
                slot.dsem, slot.dcnt = self.sem_pool.pop()
            else:
                slot.dsem = self._newsem("d")
                slot.dcnt = 0
            slot.dgen += 1
        slot.dcnt += 16
        eng.h.dma_start(out=out, in_=in_).then_inc(slot.dsem, 16)
        key = ("dma", slot.uid, slot.dgen)
        tick = (slot.dsem, slot.dcnt)
        for b in reads:
            b.r[key] = tick
        for b in writes:
            b.w[key] = tick

    def barrier(self, engs=None):
        deps = {}
        for e in self.engs:
            if e.cnt > 0:
                deps[(e.name, e.gen)] = (e.sem, e.cnt)
        for s in self.dma_bufs:
            if s.dcnt > 0:
                deps[("dma", s.uid, s.dgen)] = (s.dsem, s.dcnt)
        for e in (engs or self.engs):
            self._need(e, deps)

    def release(self, bufs):
        for b in bufs:
            if b.dsem is not None:
                self.sem_pool.append((b.dsem, b.dcnt))
                self.sem_pool.sort(key=lambda t: -t[1])
                b.dsem = None
                if b in self.dma_bufs:
                    self.dma_bufs.remove(b)


class Ctx:
    def __init__(self, nc, stack):
        self.nc = nc
        self.S = Sched(nc, stack)
        self.stack = stack
        self.n = 0
        self.live = {}

    def sb(self, st, name, shape, dtype):
        self.n += 1
        t = st.enter_context(self.nc.sbuf_tensor("%s_%d" % (name, self.n), list(shape), dtype))
        b = Buf(name)
        self.live.setdefault(id(st), []).append(b)
        return Tl(t, b)

    def end(self, st):
        self.S.barrier()
        self.S.release(self.live.pop(id(st), []))

    def ps(self, st, name, shape, dtype):
        self.n += 1
        t = st.enter_context(self.nc.psum_tensor("%s_%d" % (name, self.n), list(shape), dtype))
        return Tl(t, Buf(name))

    def dram(self, name, shape, dtype, kind="Internal"):
        t = self.nc.dram_tensor(name, list(shape), dtype, kind=kind)
        return Tl(t, Buf(name))


def x_fetch(C, xsrc, tok0, xt_tl, xbf_tl):
    S = C.S
    if xt_tl is None:
        S.dma(S.pool, xbf_tl[:, :], xsrc[tok0:tok0 + 128, :], reads=[xsrc.b], writes=[xbf_tl.b], slot=xbf_tl.b)
        return
    S.dma(S.sp, xt_tl[:, :], xsrc[tok0:tok0 + 128, :], reads=[xsrc.b], writes=[xt_tl.b], slot=xt_tl.b)
    S.op(S.pool, lambda: C.nc.gpsimd.tensor_copy(out=xbf_tl[:, :], in_=xt_tl[:, :]),
         reads=[xt_tl.b], writes=[xbf_tl.b])


def x_transpose(C, xbf_tl, ps_tr, xT_tl, col0, ident, nk=8):
    S = C.S
    trv = ps_tr.t[:, :].bitcast(BF16)
    for kc in range(nk):
        S.op(S.pe, lambda kc=kc: C.nc.tensor.transpose(trv[:, kc * 128:(kc + 1) * 128],
                                                       xbf_tl[:, kc * 128:(kc + 1) * 128], ident[:, :]),
             reads=[xbf_tl.b, ident.b], writes=[ps_tr.b], sig=(kc == nk - 1))
    S.op(S.act, lambda: C.nc.scalar.copy(out=xT_tl[:, 0:nk, col0:col0 + 128],
                                         in_=trv[:, 0:nk * 128].rearrange("p (k j) -> p k j", k=nk)),
         reads=[ps_tr.b], writes=[xT_tl.b])


def ln_epilogue(C, y_ps, xsrc, xr_tl, gbc, bbc, z_tl, st_tl, xdst, tok0, pool_free=False):
    S = C.S
    nc = C.nc
    S.dma(S.sp, xr_tl[:, :], xsrc[tok0:tok0 + 128, :], reads=[xsrc.b], writes=[xr_tl.b], slot=xr_tl.b)
    for h in range(2):
        S.op(S.dve, lambda h=h: nc.vector.scalar_tensor_tensor(
            out=z_tl[:, h * 512:(h + 1) * 512], in0=xr_tl[:, h * 512:(h + 1) * 512], scalar=ALPHA,
            in1=y_ps[h][:, :], op0=ALU.mult, op1=ALU.add),
            reads=[xr_tl.b, y_ps[h].b], writes=[z_tl.b])
    for h in range(2):
        S.op(S.dve, lambda h=h: nc.vector.bn_stats(out=st_tl[:, h * 6:(h + 1) * 6],
                                                   in_=z_tl[:, h * 512:(h + 1) * 512]),
             reads=[z_tl.b], writes=[st_tl.b])
    S.op(S.dve, lambda: nc.vector.bn_aggr(out=st_tl[:, 12:14], in_=st_tl[:, 0:12]),
         reads=[st_tl.b], writes=[st_tl.b])
    S.op(S.act, lambda: nc.scalar.activation(out=st_tl[:, 14:15], in_=st_tl[:, 13:14], func=AF.Sqrt,
                                             bias=LN_EPS, scale=1.0),
         reads=[st_tl.b], writes=[st_tl.b])
    S.op(S.dve, lambda: nc.vector.reciprocal(out=st_tl[:, 14:15], in_=st_tl[:, 14:15]),
         reads=[st_tl.b], writes=[st_tl.b])
    S.op(S.dve, lambda: nc.vector.scalar_tensor_tensor(out=st_tl[:, 15:16], in0=st_tl[:, 12:13], scalar=-1.0,
                                                       in1=st_tl[:, 14:15], op0=ALU.mult, op1=ALU.mult),
         reads=[st_tl.b], writes=[st_tl.b])
    S.op(S.act, lambda: nc.scalar.activation(out=z_tl[:, :], in_=z_tl[:, :], func=AF.Identity,
                                             bias=st_tl[:, 15:16], scale=st_tl[:, 14:15]),
         reads=[z_tl.b, st_tl.b], writes=[z_tl.b])
    S.op(S.dve, lambda: nc.vector.tensor_tensor(out=z_tl[:, :], in0=z_tl[:, :], in1=gbc[:, :], op=ALU.mult),
         reads=[z_tl.b, gbc.b], writes=[z_tl.b])
    if pool_free:
        S.op(S.dve, lambda: nc.vector.tensor_tensor(out=z_tl[:, :], in0=z_tl[:, :], in1=bbc[:, :], op=ALU.add),
             reads=[z_tl.b, bbc.b], writes=[z_tl.b])
        S.dma(S.act, xdst[tok0:tok0 + 128, :], z_tl[:, :], reads=[z_tl.b], writes=[xdst.b], slot=z_tl.b)
        return
    S.op(S.pool, lambda: nc.gpsimd.tensor_tensor(out=z_tl[:, :], in0=z_tl[:, :], in1=bbc[:, :], op=ALU.add),
         reads=[z_tl.b, bbc.b], writes=[z_tl.b])
    S.dma(S.pool, xdst[tok0:tok0 + 128, :], z_tl[:, :], reads=[z_tl.b], writes=[xdst.b], slot=z_tl.b)


def load_ln_consts(C, st, W, li, which):
    S = C.S
    gbc = C.sb(st, "gbc", [128, D], F32)
    bbc = C.sb(st, "bbc", [128, D], F32)
    S.dma(S.sp, gbc[:, :], W["ln_g"][li, which, :].partition_broadcast(128), reads=[], writes=[gbc.b], slot=gbc.b)
    S.dma(S.sp, bbc[:, :], W["ln_b"][li, which, :].partition_broadcast(128), reads=[], writes=[bbc.b], slot=bbc.b)
    return gbc, bbc


def ffn_alloc(C, st, which, have=None):
    res = dict(have or {})
    new = []
    if "w1" in which and "w1" not in res:
        res["w1"] = C.sb(st, "w1", [128, 8, 2 * FH], BF16)
        new.append("w1")
    if "w2" in which and "w2" not in res:
        res["w2"] = C.sb(st, "w2", [128, 22, D], BF16)
        new.append("w2")
    return res, new


def ffn_issue(C, W, li, tiles, names):
    S = C.S
    if "w1" in names:
        w1 = tiles["w1"]
        for k2 in range(2):
            S.dma(S.pool, w1[:, k2 * 4:(k2 + 1) * 4, :],
                  W["ffn_w_in"][li, k2 * 512:(k2 + 1) * 512, :].rearrange("(kc p) n -> p kc n", p=128),
                  reads=[], writes=[w1.b], slot=w1.b)
    if "w2" in names:
        w2 = tiles["w2"]
        S.dma(S.pool, w2[:, :, :], W["ffn_w_out"][li, :, :].rearrange("(fc p) n -> p fc n", p=128),
              reads=[], writes=[w2.b], slot=w2.b)


def ffn_sublayer(C, W, li, xsrc, xdst, SEQ, PS, ident, pre=None):
    S = C.S
    nc = C.nc
    TT = 512
    with ExitStack() as st:
        wts, newn = ffn_alloc(C, st, ("w1", "w2"), pre)
        ffn_issue(C, W, li, wts, newn)
        w1, w2 = wts["w1"], wts["w2"]
        gbc, bbc = load_ln_consts(C, st, W, li, 1)
        xts = [C.sb(st, "xt", [128, D], F32) for _ in range(2)]
        xrs = [C.sb(st, "xr", [128, D], F32) for _ in range(2)]
        xbf = [C.sb(st, "xbf", [128, D], BF16) for _ in range(4)]
        xT = [C.sb(st, "xT", [128, 8, TT], BF16) for _ in range(1)]
        aT = C.sb(st, "aT", [128, 22, TT], BF16)
        sg = [C.sb(st, "sg", [128, TT], F32) for _ in range(2)]
        z = [C.sb(st, "z", [128, D], F32) for _ in range(2)]
        stt = [C.sb(st, "stt", [128, 16], F32) for _ in range(2)]
        nt = SEQ // TT
        for s in range(4):
            x_fetch(C, xsrc, s * 128, xts[s % 2], xbf[s])
        for t in range(nt):
            xTt = xT[0]
            for s in range(4):
                x_transpose(C, xbf[s], PS[6], xTt, s * 128, ident)
            if t + 1 < nt:
                for s in range(4):
                    x_fetch(C, xsrc, (t + 1) * TT + s * 128, xts[s % 2], xbf[s])
            for fc in range(22):
                g_ps = PS[(fc % 2) * 2]
                u_ps = PS[(fc % 2) * 2 + 1]
                for kc in range(8):
                    S.op(S.pe, lambda kc=kc, fc=fc, g_ps=g_ps: nc.tensor.matmul(
                        g_ps[:, :], lhsT=w1[:, kc, fc * 128:(fc + 1) * 128], rhs=xTt[:, kc, :],
                        start=(kc == 0), stop=(kc == 7)),
                        reads=[w1.b, xTt.b], writes=[g_ps.b], sig=(kc == 7))
                for kc in range(8):
                    S.op(S.pe, lambda kc=kc, fc=fc, u_ps=u_ps: nc.tensor.matmul(
                        u_ps[:, :], lhsT=w1[:, kc, FH + fc * 128:FH + (fc + 1) * 128], rhs=xTt[:, kc, :],
                        start=(kc == 0), stop=(kc == 7)),
                        reads=[w1.b, xTt.b], writes=[u_ps.b], sig=(kc == 7))
                sgt = sg[fc % 2]
                S.op(S.act, lambda g_ps=g_ps, sgt=sgt: nc.scalar.activation(out=sgt[:, :], in_=g_ps[:, :], func=AF.Silu),
                     reads=[g_ps.b], writes=[sgt.b])
                S.op(S.dve, lambda u_ps=u_ps, sgt=sgt, fc=fc: nc.vector.tensor_tensor(
                    out=aT[:, fc, :], in0=u_ps[:, :], in1=sgt[:, :], op=ALU.mult),
                    reads=[u_ps.b, sgt.b], writes=[aT.b])
            for s in range(4):
                o_ps = [PS[4], PS[5]] if s % 2 == 0 else [PS[7], PS[6]]
                for h in range(2):
                    for fc in range(22):
                        S.op(S.pe, lambda h=h, fc=fc, s=s, o_ps=o_ps: nc.tensor.matmul(
                            o_ps[h][:, :], lhsT=aT[:, fc, s * 128:(s + 1) * 128], rhs=w2[:, fc, h * 512:(h + 1) * 512],
                            start=(fc == 0), stop=(fc == 21)),
                            reads=[aT.b, w2.b], writes=[o_ps[h].b], sig=(fc == 21))
                ln_epilogue(C, o_ps, xsrc, xrs[s % 2], gbc, bbc, z[s % 2], stt[s % 2], xdst, t * TT + s * 128)
        C.end(st)


MAGIC = 12582912.0
CW1 = 6.28125
CW2 = 2 * math.pi - CW1
PI_SAFE = 3.1415925


def rope_tables(C, st, tst, pos_in, invf_d, npart, SEQ):
    S = C.S
    nc = C.nc
    cos = C.sb(st, "cos", [npart, SEQ], F32)
    sin = C.sb(st, "sin", [npart, SEQ], F32)
    pi_ = C.sb(tst, "posi", [npart, SEQ], I32)
    a = C.sb(tst, "ang", [npart, SEQ], F32)
    k = C.sb(tst, "kk", [npart, SEQ], F32)
    fr = C.sb(tst, "fr", [npart, 1], F32)
    S.dma(S.sp, pi_[:, :], pos_in[:].partition_broadcast(npart), reads=[], writes=[pi_.b], slot=pi_.b)
    S.dma(S.sp, fr[:, :], invf_d[:, :], reads=[], writes=[fr.b], slot=fr.b)
    S.op(S.dve, lambda: nc.vector.tensor_copy(out=a[:, :], in_=pi_[:, :]), reads=[pi_.b], writes=[a.b])
    S.op(S.dve, lambda: nc.vector.tensor_scalar(out=a[:, :], in0=a[:, :], scalar1=fr[:, 0:1], scalar2=None,
                                                op0=ALU.mult), reads=[a.b, fr.b], writes=[a.b])
    for (dst, shift) in ((sin, 0.0), (cos, 0.25)):
        S.op(S.dve, lambda shift=shift: nc.vector.tensor_scalar(
            out=k[:, :], in0=a[:, :], scalar1=1.0 / (2 * math.pi), scalar2=shift, op0=ALU.mult, op1=ALU.add),
            reads=[a.b], writes=[k.b])
        S.op(S.dve, lambda: nc.vector.tensor_scalar(out=k[:, :], in0=k[:, :], scalar1=MAGIC, scalar2=MAGIC,
                                                    op0=ALU.add, op1=ALU.subtract), reads=[k.b], writes=[k.b])
        S.op(S.dve, lambda dst=dst: nc.vector.scalar_tensor_tensor(
            out=dst[:, :], in0=k[:, :], scalar=-CW1, in1=a[:, :], op0=ALU.mult, op1=ALU.add),
            reads=[a.b, k.b], writes=[dst.b])
        S.op(S.dve, lambda dst=dst: nc.vector.scalar_tensor_tensor(
            out=dst[:, :], in0=k[:, :], scalar=-CW2, in1=dst[:, :], op0=ALU.mult, op1=ALU.add),
            reads=[dst.b, k.b], writes=[dst.b])
        S.op(S.dve, lambda dst=dst, shift=shift: nc.vector.tensor_scalar(
            out=dst[:, :], in0=dst[:, :], scalar1=shift * 2 * math.pi, scalar2=PI_SAFE, op0=ALU.add, op1=ALU.min),
            reads=[dst.b], writes=[dst.b])
        S.op(S.dve, lambda dst=dst: nc.vector.tensor_scalar(
            out=dst[:, :], in0=dst[:, :], scalar1=-PI_SAFE, scalar2=None, op0=ALU.max),
            reads=[dst.b], writes=[dst.b])
        S.op(S.act, lambda dst=dst: nc.scalar.activation(out=dst[:, :], in_=dst[:, :], func=AF.Sin),
             reads=[dst.b], writes=[dst.b])
    return cos, sin


def load_w1(C, dst_tl, dst_ap, src2d):
    C.S.dma(C.S.pool, dst_ap, src2d.rearrange("(kc p) n -> p kc n", p=128), reads=[], writes=[dst_tl.b], slot=dst_tl.b)


def load_w_bf16(C, dst_tl, dst_fn, src_fn, nchunks):
    S = C.S
    for kc in range(nchunks):
        S.dma(S.pool, dst_fn(kc), src_fn(kc), reads=[], writes=[dst_tl.b], slot=dst_tl.b)


def mla_sublayer(C, W, li, xsrc, xdst, SEQ, PS, ident, CONST, SCR, hook=None):
    S = C.S
    nc = C.nc
    j = li // 3
    TT = 512
    nt = SEQ // TT
    nkc = SEQ // 128
    SC = 192.0 ** -0.5
    oT_d = SCR["oT"]
    with ExitStack() as st:
        cqn = C.sb(st, "cqn", [128, 4, SEQ], BF16)
        ckvn = C.sb(st, "ckvn", [128, 2, SEQ], BF16)
        krot = C.sb(st, "krot", [128, SEQ], BF16)
        S.op(S.pool, lambda: nc.gpsimd.memset(krot[64:128, :], 0.0), writes=[krot.b])
        ones = C.sb(st, "ones", [128, 128], BF16)
        S.op(S.pool, lambda: nc.gpsimd.memset(ones[:, :], 1.0), writes=[ones.b])
        rc = SCR.get("rope64")
        if rc is not None and SCR.get("rope64_ready"):
            cos = C.sb(st, "cos", [64, SEQ], F32)
            sin = C.sb(st, "sin", [64, SEQ], F32)
            S.dma(S.sp, cos[:, :], rc[0, :, :], reads=[rc.b], writes=[cos.b], slot=cos.b)
            S.dma(S.sp, sin[:, :], rc[1, :, :], reads=[rc.b], writes=[sin.b], slot=sin.b)
        else:
            with ExitStack() as tst:
                cos, sin = rope_tables(C, st, tst, CONST["pos"], CONST["invf64"], 64, SEQ)
                C.end(tst)
            if rc is not None:
                S.dma(S.act, rc[0, :, :], cos[:, :], reads=[cos.b], writes=[rc.b], slot=cos.b)
                S.dma(S.act, rc[1, :, :], sin[:, :], reads=[sin.b], writes=[rc.b], slot=sin.b)
                SCR["rope64_ready"] = True
        with ExitStack() as pa:
            wd = C.sb(pa, "wd", [128, 8, 960], BF16)
            S.op(S.pool, lambda: nc.gpsimd.memset(wd[:, :, 896:960], 0.0), writes=[wd.b])
            load_w1(C, wd, wd[:, :, 0:832], W["mla_w_down"][j])
            S.op(S.act, lambda: nc.scalar.mul(out=wd[:, :, 832:864], in_=wd[:, :, 800:832], mul=-1.0),
                 reads=[wd.b], writes=[wd.b])
            S.op(S.act, lambda: nc.scalar.copy(out=wd[:, :, 864:896], in_=wd[:, :, 768:800]),
                 reads=[wd.b], writes=[wd.b])
            qg = C.sb(pa, "qg", [128, 4], F32)
            kvg = C.sb(pa, "kvg", [128, 2], F32)
            for c in range(4):
                S.dma(S.sp, qg[:, c:c + 1], W["mla_q_norm"][j, c * 128:(c + 1) * 128].rearrange("(p o) -> p o", o=1),
                      reads=[], writes=[qg.b], slot=qg.b)
            for c in range(2):
                S.dma(S.sp, kvg[:, c:c + 1], W["mla_kv_norm"][j, c * 128:(c + 1) * 128].rearrange("(p o) -> p o", o=1),
                      reads=[], writes=[kvg.b], slot=kvg.b)
            xts = [C.sb(pa, "xt", [128, D], F32) for _ in range(2)]
            xbf = [C.sb(pa, "xbf", [128, D], BF16) for _ in range(4)]
            xT = C.sb(pa, "xT", [128, 8, TT], BF16)
            raw = C.sb(pa, "raw", [128, 6, TT], F32)
            sq = [C.sb(pa, "sq", [128, TT], BF16) for _ in range(2)]
            rs = [C.sb(pa, "rs", [128, TT], F32) for _ in range(2)]
            t1 = C.sb(pa, "t1", [64, TT], F32)
            t2 = C.sb(pa, "t2", [64, TT], F32)
            for s in range(4):
                x_fetch(C, xsrc, s * 128, xts[s % 2], xbf[s])
            for t in range(nt):
                tsl = slice(t * TT, (t + 1) * TT)
                for s in range(4):
                    x_transpose(C, xbf[s], PS[6], xT, s * 128, ident)
                if t + 1 < nt:
                    for s in range(4):
                        x_fetch(C, xsrc, (t + 1) * TT + s * 128, xts[s % 2], xbf[s])
                pend_ss = []
                for oc in range(6):
                    d_ps = PS[oc % 2]
                    ss_ps = PS[2] if oc < 4 else PS[3]
                    for kc in range(8):
                        S.op(S.pe, lambda kc=kc, oc=oc, d_ps=d_ps: nc.tensor.matmul(
                            d_ps[:, :], lhsT=wd[:, kc, oc * 128:(oc + 1) * 128], rhs=xT[:, kc, :],
                            start=(kc == 0), stop=(kc == 7)), reads=[wd.b, xT.b], writes=[d_ps.b], sig=(kc == 7))
                    S.op(S.act, lambda oc=oc, d_ps=d_ps: nc.scalar.copy(out=raw[:, oc, :], in_=d_ps[:, :]),
                         reads=[d_ps.b], writes=[raw.b])
                    sqt = sq[oc % 2]
                    S.op(S.act, lambda d_ps=d_ps, sqt=sqt: nc.scalar.activation(out=sqt[:, :], in_=d_ps[:, :], func=AF.Square),
                         reads=[d_ps.b], writes=[sqt.b])
                    first = oc in (0, 4)
                    last = oc in (3, 5)
                    if pend_ss:
                        pend_ss.pop()()
                    pend_ss.append(lambda sqt=sqt, ss_ps=ss_ps, first=first, last=last: S.op(
                        S.pe, lambda: nc.tensor.matmul(ss_ps[:, :], lhsT=ones[:, :], rhs=sqt[:, :], start=first, stop=last),
                        reads=[ones.b, sqt.b], writes=[ss_ps.b], sig=True))
                pend_ss.pop()()
                for (ss_ps, rst, n) in ((PS[2], rs[0], 512.0), (PS[3], rs[1], 256.0)):
                    S.op(S.act, lambda ss_ps=ss_ps, rst=rst, n=n: nc.scalar.activation(
                        out=rst[:, :], in_=ss_ps[:, :], func=AF.Sqrt, bias=RMS_EPS, scale=1.0 / n),
                        reads=[ss_ps.b], writes=[rst.b])
                    S.op(S.dve, lambda rst=rst: nc.vector.reciprocal(out=rst[:, :], in_=rst[:, :]),
                         reads=[rst.b], writes=[rst.b])
                for c in range(4):
                    S.op(S.dve, lambda c=c: nc.vector.scalar_tensor_tensor(
                        out=cqn[:, c, tsl], in0=raw[:, c, :], scalar=qg[:, c:c + 1], in1=rs[0][:, :],
                        op0=ALU.mult, op1=ALU.mult), reads=[raw.b, qg.b, rs[0].b], writes=[cqn.b])
                for c in range(2):
                    S.op(S.dve, lambda c=c: nc.vector.scalar_tensor_tensor(
                        out=ckvn[:, c, tsl], in0=raw[:, 4 + c, :], scalar=kvg[:, c:c + 1], in1=rs[1][:, :],
                        op0=ALU.mult, op1=ALU.mult), reads=[raw.b, kvg.b, rs[1].b], writes=[ckvn.b])
                for (ps_, c0) in ((PS[4], 768), (PS[5], 832)):
                    for kc in range(8):
                        S.op(S.pe, lambda kc=kc, ps_=ps_, c0=c0: nc.tensor.matmul(
                            ps_[:, :], lhsT=wd[:, kc, c0:c0 + 128], rhs=xT[:, kc, :],
                            start=(kc == 0), stop=(kc == 7)), reads=[wd.b, xT.b], writes=[ps_.b], sig=(kc == 7))
                S.op(S.dve, lambda: nc.vector.tensor_tensor(out=t1[:, :], in0=PS[4][0:64, :], in1=cos[:, tsl], op=ALU.mult),
                     reads=[PS[4].b, cos.b], writes=[t1.b])
                S.op(S.dve, lambda: nc.vector.tensor_tensor(out=t2[:, :], in0=PS[5][0:64, :], in1=sin[:, tsl], op=ALU.mult),
                     reads=[PS[5].b, sin.b], writes=[t2.b])
                S.op(S.pool, lambda: nc.gpsimd.tensor_tensor(out=krot[0:64, tsl], in0=t1[:, :], in1=t2[:, :], op=ALU.add),
                     reads=[t1.b, t2.b], writes=[krot.b])
            C.end(pa)
        with ExitStack() as pb:
            wq = C.sb(pb, "wq", [128, 4, 2112], BF16)
            S.op(S.pool, lambda: nc.gpsimd.memset(wq[:, :, 2048:2112], 0.0), writes=[wq.b])
            wkv = C.sb(pb, "wkv", [128, 2, 2048], BF16)
            load_w1(C, wq, wq[:, :, 0:1536], W["mla_w_uq"][j])
            load_w1(C, wkv, wkv[:, :, :], W["mla_w_ukv"][j])
            for h in range(8):
                S.op(S.act, lambda h=h: nc.scalar.mul(out=wq[:, :, 1536 + h * 64:1536 + h * 64 + 32],
                                                      in_=wq[:, :, h * 192 + 160:h * 192 + 192], mul=-1.0),
                     reads=[wq.b], writes=[wq.b])
                S.op(S.act, lambda h=h: nc.scalar.copy(out=wq[:, :, 1536 + h * 64 + 32:1536 + h * 64 + 64],
                                                       in_=wq[:, :, h * 192 + 128:h * 192 + 160]),
                     reads=[wq.b], writes=[wq.b])
            kT = [C.sb(pb, "kT", [128, SEQ], BF16) for _ in range(2)]
            vv = [C.sb(pb, "vv", [128, nkc, 128], BF16) for _ in range(2)]
            qn = [C.sb(pb, "qn", [128, TT], BF16) for _ in range(2)]
            qr = [C.sb(pb, "qr", [128, TT], BF16) for _ in range(2)]
            for q_ in qr:
                S.op(S.pool, lambda q_=q_: nc.gpsimd.memset(q_[64:128, :], 0.0), writes=[q_.b])
            u1 = C.sb(pb, "u1", [64, TT], F32)
            u2 = C.sb(pb, "u2", [64, TT], F32)
            pT = [C.sb(pb, "pT", [128, TT], BF16) for _ in range(6)]
            rinv = C.sb(pb, "rinv", [128, TT], F32)
            on = [C.sb(pb, "on", [128, TT], BF16) for _ in range(2)]
            units = [(h, qt) for h in range(8) for qt in range(nt)]

            def prep(u):
                h, qt = units[u]
                if qt == 0:
                    kTh = kT[h % 2]
                    vh = vv[h % 2]
                    for t in range(nt):
                        ps_ = PS[6 + t % 2]
                        for c in range(2):
                            S.op(S.pe, lambda c=c, t=t, ps_=ps_: nc.tensor.matmul(
                                ps_[:, :], lhsT=wkv[:, c, h * 256:h * 256 + 128], rhs=ckvn[:, c, t * TT:(t + 1) * TT],
                                start=(c == 0), stop=(c == 1)), reads=[wkv.b, ckvn.b], writes=[ps_.b], sig=(c == 1))
                        S.op(S.act, lambda t=t, ps_=ps_: nc.scalar.copy(out=kTh[:, t * TT:(t + 1) * TT], in_=ps_[:, :]),
                             reads=[ps_.b], writes=[kTh.b])
                    for t in range(nt):
                        ps_ = PS[6 + t % 2]
                        for q4 in range(4):
                            tk = t * 4 + q4
                            for c in range(2):
                                S.op(S.pe, lambda c=c, tk=tk, q4=q4, ps_=ps_: nc.tensor.matmul(
                                    ps_[:, q4 * 128:(q4 + 1) * 128], lhsT=ckvn[:, c, tk * 128:(tk + 1) * 128],
                                    rhs=wkv[:, c, h * 256 + 128:h * 256 + 256], start=(c == 0), stop=(c == 1)),
                                    reads=[wkv.b, ckvn.b], writes=[ps_.b], sig=(c == 1 and q4 == 3))
                        S.op(S.dve, lambda t=t, ps_=ps_: nc.vector.tensor_copy(
                            out=vh[:, t * 4:(t + 1) * 4, :], in_=ps_[:, :].rearrange("p (a b) -> p a b", a=4)),
                            reads=[ps_.b], writes=[vh.b])
                qsl = slice(qt * TT, (qt + 1) * TT)
                qnt = qn[u % 2]
                qrt = qr[u % 2]
                for c in range(4):
                    S.op(S.pe, lambda c=c: nc.tensor.matmul(
                        PS[6][:, :], lhsT=wq[:, c, h * 192:h * 192 + 128], rhs=cqn[:, c, qsl],
                        start=(c == 0), stop=(c == 3)), reads=[wq.b, cqn.b], writes=[PS[6].b], sig=(c == 3))
                S.op(S.act, lambda: nc.scalar.copy(out=qnt[:, :], in_=PS[6][:, :]), reads=[PS[6].b], writes=[qnt.b])
                for c in range(4):
                    S.op(S.pe, lambda c=c: nc.tensor.matmul(
                        PS[7][:, :], lhsT=wq[:, c, h * 192 + 128:h * 192 + 256], rhs=cqn[:, c, qsl],
                        start=(c == 0), stop=(c == 3)), reads=[wq.b, cqn.b], writes=[PS[7].b], sig=(c == 3))
                S.op(S.dve, lambda: nc.vector.tensor_tensor(out=u1[:, :], in0=PS[7][0:64, :], in1=cos[:, qsl], op=ALU.mult),
                     reads=[PS[7].b, cos.b], writes=[u1.b])
                for c in range(4):
                    S.op(S.pe, lambda c=c: nc.tensor.matmul(
                        PS[7][:, :], lhsT=wq[:, c, 1536 + h * 64:1536 + h * 64 + 128], rhs=cqn[:, c, qsl],
                        start=(c == 0), stop=(c == 3)), reads=[wq.b, cqn.b], writes=[PS[7].b], sig=(c == 3))
                S.op(S.dve, lambda: nc.vector.tensor_tensor(out=u2[:, :], in0=PS[7][0:64, :], in1=sin[:, qsl], op=ALU.mult),
                     reads=[PS[7].b, sin.b], writes=[u2.b])
                S.op(S.pool, lambda: nc.gpsimd.tensor_tensor(out=qrt[0:64, :], in0=u1[:, :], in1=u2[:, :], op=ALU.add),
                     reads=[u1.b, u2.b], writes=[qrt.b])

            ones32 = C.sb(pb, "ones32", [128, 128], F32)
            S.op(S.pool, lambda: nc.gpsimd.memset(ones32[:, :], 1.0), writes=[ones32.b])
            NA, NB = 3, 2
            accA = [[C.sb(pb, "accA", [128, TT], F32) for _ in range(NA)] for _ in range(2)]
            prep(0)
            pi = [0]
            sidx = [0]
            sbank = {}
            pend = []

            def qk(u, kc):
                h, qt = units[u]
                kTh = kT[h % 2]
                s_ps = PS[sidx[0] % 3]
                sbank[(u, kc)] = s_ps
                sidx[0] += 1
                ksl = slice(kc * 128, (kc + 1) * 128)
                S.op(S.pe, lambda: nc.tensor.matmul(s_ps[:, :], lhsT=kTh[:, ksl], rhs=qn[u % 2][:, :], start=True, stop=False),
                     reads=[kTh.b, qn[u % 2].b], writes=[s_ps.b], sig=False)
                S.op(S.pe, lambda: nc.tensor.matmul(s_ps[:, :], lhsT=krot[:, ksl], rhs=qr[u % 2][:, :], start=False, stop=True),
                     reads=[krot.b, qr[u % 2].b], writes=[s_ps.b], sig=True)

            def finish(u):
                h, qt = units[u]
                o_ps = PS[3 + u % 2]
                sum_ps = PS[5]
                for a_ in accA[u % 2][1:min(NA, used[u][0])]:
                    S.op(S.dve, lambda a_=a_: nc.vector.tensor_tensor(out=accA[u % 2][0][:, :], in0=accA[u % 2][0][:, :], in1=a_[:, :],
                                                                      op=ALU.add), reads=[a_.b, accA[u % 2][0].b], writes=[accA[u % 2][0].b])
                aA = accA[u % 2][0]
                S.op(S.pe, lambda: nc.tensor.matmul(sum_ps[:, :], lhsT=ones32[:, :], rhs=aA[:, :], start=(used[u][1] == 0), stop=True),
                     reads=[ones32.b, aA.b], writes=[sum_ps.b], sig=True)
                ont = on[u % 2]
                S.op(S.dve, lambda: nc.vector.reciprocal(out=rinv[:, :], in_=sum_ps[:, :]),
                     reads=[sum_ps.b], writes=[rinv.b])
                S.op(S.dve, lambda: nc.vector.tensor_tensor(out=ont[:, :], in0=o_ps[:, :], in1=rinv[:, :], op=ALU.mult),
                     reads=[o_ps.b, rinv.b], writes=[ont.b])
                S.dma(S.sp, oT_d[qt * 4:(qt + 1) * 4, :, h, :].rearrange("c p t -> p c t"),
                      ont[:, :].rearrange("p (c t) -> p c t", c=4), reads=[ont.b], writes=[oT_d.b], slot=ont.b)

            seq = [(u, kc) for u in range(len(units)) for kc in range(nkc)]
            LOOK = 2
            cntA, cntB = [0], [0]
            used = {}
            for i in range(min(LOOK, len(seq))):
                qk(*seq[i])
            for i, (u, kc) in enumerate(seq):
                h, qt = units[u]
                vh = vv[h % 2]
                o_ps = PS[3 + u % 2]
                if kc == 0:
                    cntA[0] = 0
                    cntB[0] = 0
                if kc == min(4, nkc - 1 - LOOK) and u + 1 < len(units):
                    prep(u + 1)
                if kc == min(2, nkc - 1) and pend:
                    finish(pend.pop())
                if i + LOOK < len(seq):
                    qk(*seq[i + LOOK])
                s_ps = sbank.pop((u, kc))
                pTt = pT[pi[0] % 6]
                pi[0] += 1
                S.op(S.act, lambda s_ps=s_ps, pTt=pTt: nc.scalar.activation(
                    out=pTt[:, :], in_=s_ps[:, :], func=AF.Exp, scale=SC),
                    reads=[s_ps.b], writes=[pTt.b])
                S.op(S.pe, lambda pTt=pTt, kc=kc, vh=vh, o_ps=o_ps: nc.tensor.matmul(
                    o_ps[:, :], lhsT=vh[:, kc, :], rhs=pTt[:, :], start=(kc == 0), stop=(kc == nkc - 1)),
                    reads=[vh.b, pTt.b], writes=[o_ps.b], sig=True)
                sum_ps = PS[5]
                if kc % 8 == 7:
                    S.op(S.pe, lambda pTt=pTt, first=(cntB[0] == 0): nc.tensor.matmul(
                        sum_ps[:, :], lhsT=ones[:, :], rhs=pTt[:, :], start=first, stop=False),
                        reads=[ones.b, pTt.b], writes=[sum_ps.b], sig=True)
                    cntB[0] += 1
                else:
                    acc = accA[u % 2][cntA[0] % NA]
                    first = cntA[0] < NA
                    cntA[0] += 1
                    if first:
                        S.op(S.dve, lambda acc=acc, pTt=pTt: nc.vector.tensor_copy(out=acc[:, :], in_=pTt[:, :]),
                             reads=[pTt.b], writes=[acc.b])
                    else:
                        S.op(S.dve, lambda acc=acc, pTt=pTt: nc.vector.tensor_tensor(out=acc[:, :], in0=acc[:, :], in1=pTt[:, :], op=ALU.add),
                             reads=[pTt.b, acc.b], writes=[acc.b])
                if kc == nkc - 1:
                    used[u] = (cntA[0], cntB[0])
                    pend.append(u)
            while pend:
                finish(pend.pop())
            C.end(pb)
        C.end(st)
    if hook is not None:
        hook("alloc")
    with ExitStack() as pc:
        wo = C.sb(pc, "wo", [128, 8, D], BF16)
        load_w1(C, wo, wo[:, :, :], W["mla_w_o"][j])
        gbc, bbc = load_ln_consts(C, pc, W, li, 0)
        if hook is not None:
            hook("load")
        oTs = [C.sb(pc, "oTs", [128, 8, 128], BF16) for _ in range(3)]
        xrs = [C.sb(pc, "xr", [128, D], F32) for _ in range(2)]
        z = [C.sb(pc, "z", [128, D], F32) for _ in range(2)]
        stt = [C.sb(pc, "stt", [128, 16], F32) for _ in range(2)]
        for tk in range(nkc):
            ot = oTs[tk % 3]
            S.dma(S.sp, ot[:, :, :], oT_d[tk, :, :, :], reads=[oT_d.b], writes=[ot.b], slot=ot.b)
            o_ps = [PS[(tk % 2) * 2], PS[(tk % 2) * 2 + 1]]
            for hf in range(2):
                for h in range(8):
                    S.op(S.pe, lambda hf=hf, h=h, ot=ot, o_ps=o_ps: nc.tensor.matmul(
                        o_ps[hf][:, :], lhsT=ot[:, h, :], rhs=wo[:, h, hf * 512:(hf + 1) * 512],
                        start=(h == 0), stop=(h == 7)), reads=[ot.b, wo.b], writes=[o_ps[hf].b], sig=(h == 7))
            ln_epilogue(C, o_ps, xsrc, xrs[tk % 2], gbc, bbc, z[tk % 2], stt[tk % 2], xdst, tk * 128,
                        pool_free=(hook is not None and tk < 16))
        C.end(pc)


def lin_sublayer(C, W, li, xsrc, xdst, SEQ, PS, ident, CONST, SCR, kind, hook=None):
    S = C.S
    nc = C.nc
    ret = (kind == "ret")
    H = 4
    DKC = 2 if ret else 1
    DV = 512 if ret else 256
    NQ = H * DKC
    QT = NQ * 128
    VT = H * DV
    w_in = W["ret_w_in"] if ret else W["gla_w_in"]
    w_o = W["ret_w_o"] if ret else W["gla_w_o"]
    KOFF, VOFF, GOFF = QT, 2 * QT, 2 * QT + VT
    N = SEQ // 128
    y_d, R_d, cs_d, V_d = SCR["y"], SCR["R"], SCR["cs"], SCR["V"]
    with ExitStack() as st:
        mask = C.sb(st, "mask", [128, 256], F32)
        S.dma(S.sp, mask[:, :], CONST["mask"][:, :], reads=[], writes=[mask.b], slot=mask.b)
        if ret:
            with ExitStack() as tst:
                cos, sin = rope_tables(C, tst, tst, CONST["pos"], CONST["invf128"], 128, SEQ)
                S.dma(S.sp, cs_d[0, :, :], cos[:, :], reads=[cos.b], writes=[cs_d.b], slot=cos.b)
                S.dma(S.sp, cs_d[1, :, :], sin[:, :], reads=[sin.b], writes=[cs_d.b], slot=sin.b)
                C.end(tst)
            io = C.sb(st, "io", [128, 4, 128], F32)
            S.dma(S.sp, io[:, :, :], CONST["io"][:, :, :], reads=[], writes=[io.b], slot=io.b)
            lg = C.sb(st, "lg", [128, 8], F32)
            lpos = C.sb(st, "lpos", [128, 8], F32)
            lneg = C.sb(st, "lneg", [128, 8], F32)
            S.dma(S.sp, lg[:, :], W["ret_decay_logit"][0].rearrange("a b -> (a b)").partition_broadcast(128),
                  reads=[], writes=[lg.b], slot=lg.b)
            S.op(S.act, lambda: nc.scalar.activation(out=lg[:, :], in_=lg[:, :], func=AF.Exp, scale=-1.0),
                 reads=[lg.b], writes=[lg.b])
            S.op(S.act, lambda: nc.scalar.activation(out=lpos[:, :], in_=lg[:, :], func=AF.Ln, bias=1.0, scale=1.0),
                 reads=[lg.b], writes=[lpos.b])
            S.op(S.act, lambda: nc.scalar.mul(out=lneg[:, :], in_=lpos[:, :], mul=-1.0), reads=[lpos.b], writes=[lneg.b])
            tabs = C.sb(st, "tabs", [128, 6, H, 128], F32)
            edge = C.sb(st, "edge", [128, 8], F32)
            kb = C.sb(st, "kb", [128, 1], F32)
            S.op(S.pool, lambda: nc.gpsimd.memset(kb[:, :], math.log(256.0 ** -0.5)), writes=[kb.b])
            zb = C.sb(st, "zb", [128, 1], F32)
            S.op(S.pool, lambda: nc.gpsimd.memset(zb[:, :], 0.0), writes=[zb.b])
            for z in range(2):
                for vi, (ioi, sgn, bias) in enumerate(((0 if z == 0 else 2, lneg, zb), (0 if z == 0 else 2, lpos, kb),
                                                       (1 if z == 0 else 3, lneg, kb))):
                    for h in range(H):
                        col = z * 4 + h
                        S.op(S.act, lambda z=z, vi=vi, h=h, ioi=ioi, sgn=sgn, col=col, bias=bias: nc.scalar.activation(
                            out=tabs[:, z * 3 + vi, h, :], in_=io[:, ioi, :], func=AF.Exp, scale=sgn[:, col:col + 1],
                            bias=bias[:, 0:1]), reads=[io.b, sgn.b, bias.b], writes=[tabs.b])
            S.op(S.act, lambda: nc.scalar.activation(out=edge[:, :], in_=lneg[:, :], func=AF.Exp, scale=128.0),
                 reads=[lneg.b], writes=[edge.b])
            dij = C.sb(st, "dij", [128, 128], F32)
            pcol = C.sb(st, "pcol", [128, 2], F32)
            S.dma(S.sp, dij[:, :], CONST["dij"][:, :], reads=[], writes=[dij.b], slot=dij.b)
            S.dma(S.sp, pcol[:, :], CONST["pcol"][:, :], reads=[], writes=[pcol.b], slot=pcol.b)
            Dp = C.sb(st, "Dp", [128, H, 128], F32)
            dtmp = C.sb(st, "dtmp", [128, 2, 128], F32)
            decK = C.sb(st, "decK", [128, 8], F32)
            for h in range(H):
                S.op(S.act, lambda h=h: nc.scalar.activation(out=decK[:, h:h + 1], in_=lpos[:, h:h + 1], func=AF.Exp,
                                                             scale=pcol[:, 1:2], bias=lpos[:, h:h + 1]),
                     reads=[lpos.b, pcol.b], writes=[decK.b])
                S.op(S.dve, lambda h=h: nc.vector.tensor_scalar(out=dtmp[:, 0, :], in0=mask[:, 0:128], scalar1=decK[:, h:h + 1],
                                                                scalar2=256.0 ** -0.5, op0=ALU.mult, op1=ALU.mult),
                     reads=[mask.b, decK.b], writes=[dtmp.b])
                S.op(S.act, lambda h=h: nc.scalar.activation(out=dtmp[:, 1, :], in_=dij[:, :], func=AF.Exp,
                                                             scale=lpos[:, 4 + h:5 + h]), reads=[dij.b, lpos.b], writes=[dtmp.b])
                S.op(S.dve, lambda: nc.vector.tensor_tensor(out=dtmp[:, 1, :], in0=dtmp[:, 1, :], in1=mask[:, 128:256], op=ALU.mult),
                     reads=[dtmp.b, mask.b], writes=[dtmp.b])
                S.op(S.dve, lambda h=h: nc.vector.tensor_tensor(out=dtmp[:, 1, :], in0=dtmp[:, 1, :], in1=tabs[:, 1, h, :], op=ALU.mult),
                     reads=[dtmp.b, tabs.b], writes=[dtmp.b])
                S.op(S.dve, lambda h=h: nc.vector.tensor_tensor(out=Dp[:, h, :], in0=dtmp[:, 0, :], in1=dtmp[:, 1, :], op=ALU.add),
                     reads=[dtmp.b], writes=[Dp.b])
            for z in range(2):
                for h in range(H):
                    col = z * 4 + h
                    S.op(S.act, lambda z=z, col=col: nc.scalar.activation(out=decK[:, col:col + 1], in_=lneg[:, col:col + 1], func=AF.Exp,
                                                                          scale=pcol[:, z:z + 1], bias=kb[:, 0:1]),
                         reads=[lneg.b, pcol.b, kb.b, decK.b], writes=[decK.b])
        else:
            ones32 = C.sb(st, "ones32", [128, 128], F32)
            S.op(S.pool, lambda: nc.gpsimd.memset(ones32[:, :], 1.0), writes=[ones32.b])
            wa1 = C.sb(st, "wa1", [128, 2, 8, 16], BF16)
            wa2 = C.sb(st, "wa2", [16, 2, 512], BF16)
            ba = C.sb(st, "ba", [128, 8], F32)
            for z in range(2):
                S.dma(S.pool, wa1[:, z, :, :], W["gla_w_a1"][0, z].rearrange("(kc p) r -> p kc r", p=128),
                      reads=[], writes=[wa1.b], slot=wa1.b)
                S.dma(S.pool, wa2[:, z, :], W["gla_w_a2"][0, z], reads=[], writes=[wa2.b], slot=wa2.b)
                for h in range(H):
                    S.dma(S.sp, ba[:, z * 4 + h:z * 4 + h + 1],
                          W["gla_b_a"][0, z, h * 128:(h + 1) * 128].rearrange("(p o) -> p o", o=1),
                          reads=[], writes=[ba.b], slot=ba.b)

        def phase12(ph, mode):
            if mode == 1:
                wA = C.sb(ph, "wkv", [128, 8, QT + VT], BF16)
                load_w1(C, wA, wA[:, :, :], w_in[0, :, KOFF:GOFF])
                KC0, QC0 = 0, None
            else:
                wA = C.sb(ph, "wqk", [128, 8, 2 * QT], BF16)
                load_w1(C, wA, wA[:, :, :], w_in[0, :, 0:2 * QT])
                KC0, QC0 = QT, 0
            xts = [None, None]
            NXB = 6
            xbf = [C.sb(ph, "xbf", [128, D], BF16) for _ in range(NXB)]
            xT4 = C.sb(ph, "xT4", [128, 8, 512], BF16)
            rawK = C.sb(ph, "rawK", [128, NQ, 512], F32)
            rawQ = C.sb(ph, "rawQ", [128, NQ, 512], F32) if mode == 2 else None
            tmp = C.sb(ph, "tmp", [128, NQ, 128], F32)
            rotk = C.sb(ph, "rotk", [128, NQ, 128], F32) if ret else None
            rotq = C.sb(ph, "rotq", [128, NQ, 128], F32)
            if ret:
                names = ("Kp",) if mode == 1 else ("Qf", "Qb", "Kp")
            else:
                names = ("KSb",) if mode == 1 else ("Qf", "Qb", "Kf", "Kb", "KSf")
            var = {nm: [C.sb(ph, nm, [128, NQ, 128], BF16) for _ in range(2)] for nm in names}
            kstm1 = C.sb(ph, "kstm", [128, QT], BF16)
            kstm = [kstm1, kstm1]
            vtm = [C.sb(ph, "vtm", [128, VT], BF16) for _ in range(2)]
            cst = [C.sb(ph, "cst", [128, 2, H, 128], F32) for _ in range(2)] if ret else None
            st32 = C.sb(ph, "st32", [128, NQ, DV], F32)
            stbf = [C.sb(ph, "stbf", [128, NQ, DV], BF16) for _ in range(2)]
            rbf = [C.sb(ph, "rbf", [128, NQ, DV], BF16) for _ in range(2)] if mode == 2 else None
            msk = [C.sb(ph, "msk", [128, 256], BF16) for _ in range(2)]
            yst = [C.sb(ph, "yst", [128, VT], F32) for _ in range(1)] if mode == 2 else None
            S.op(S.pool, lambda: nc.gpsimd.memset(st32[:, :, :], 0.0), writes=[st32.b])
            S.op(S.pool, lambda: nc.gpsimd.memset(stbf[0][:, :, :], 0.0), writes=[stbf[0].b])
            if not ret:
                gl = C.sb(ph, "gl", [128, H, 128], F32)
                gL = C.sb(ph, "gL", [128, H, 128], F32)
                gE = {k: C.sb(ph, "gE" + k, [128, H, 128], F32) for k in ("a", "b", "c")}
                gtl = C.sb(ph, "gtl", [16, 128], BF16)
                gedge = [C.sb(ph, "gedge", [128, 8], F32) for _ in range(2)]
                ginv = C.sb(ph, "ginv", [128, 4], F32)
            order = list(range(N)) if mode == 2 else list(range(N - 1, -1, -1))
            for i in range(min(NXB, N)):
                x_fetch(C, xsrc, order[i] * 128, xts[i % 2], xbf[i % NXB])
            alt = [0]

            def ew(fn, reads, writes, force=None):
                if force is None:
                    eng = S.dve if alt[0] % 2 == 0 else S.pool
                    alt[0] += 1
                else:
                    eng = force
                h = nc.vector if eng is S.dve else nc.gpsimd
                S.op(eng, lambda: fn(h), reads=reads, writes=writes)

            def project(coff, dst):
                for oc in range(NQ):
                    ps_ = PS[oc % 2]
                    for kc in range(8):
                        S.op(S.pe, lambda oc=oc, kc=kc, ps_=ps_: nc.tensor.matmul(
                            ps_[:, :], lhsT=wA[:, kc, coff + oc * 128:coff + (oc + 1) * 128], rhs=xT4[:, kc, :],
                            start=(kc == 0), stop=(kc == 7)), reads=[wA.b, xT4.b], writes=[ps_.b], sig=(kc == 7))
                    S.op(S.act, lambda oc=oc, ps_=ps_: nc.scalar.copy(out=dst[:, oc, :], in_=ps_[:, :]),
                         reads=[ps_.b], writes=[dst.b])

            def rotary(src, csl, dst, cs, out=None):
                raw = src
                sv = src[:, :, csl].rearrange("p (h c) t -> p h c t", c=2)
                dv_ = dst[:, :, :].rearrange("p (h c) t -> p h c t", c=2)
                tv = tmp[:, :, :].rearrange("p (h c) t -> p h c t", c=2)
                cosb = cs[:, 0, :, :]
                sinb = cs[:, 1, :, :]
                x1, x2 = sv[:, :, 0, :], sv[:, :, 1, :]
                ew(lambda e: e.tensor_tensor(out=dv_[:, :, 0, :], in0=x1, in1=cosb, op=ALU.mult), [raw.b, cs.b], [dst.b])
                ew(lambda e: e.tensor_tensor(out=tv[:, :, 0, :], in0=x2, in1=sinb, op=ALU.mult), [raw.b, cs.b], [tmp.b])
                ew(lambda e: e.tensor_tensor(out=dv_[:, :, 1, :], in0=x1, in1=sinb, op=ALU.mult), [raw.b, cs.b], [dst.b])
                ew(lambda e: e.tensor_tensor(out=tv[:, :, 1, :], in0=x2, in1=cosb, op=ALU.mult), [raw.b, cs.b], [tmp.b])
                o_ = dst if out is None else out
                ov = o_[:, :, :].rearrange("p (h c) t -> p h c t", c=2)
                ew(lambda e: e.tensor_tensor(out=ov[:, :, 0, :], in0=dv_[:, :, 0, :], in1=tv[:, :, 0, :], op=ALU.subtract),
                   [dst.b, tmp.b], [o_.b])
                ew(lambda e: e.tensor_tensor(out=ov[:, :, 1, :], in0=dv_[:, :, 1, :], in1=tv[:, :, 1, :], op=ALU.add),
                   [dst.b, tmp.b], [o_.b])

            def variant(src_ap, src_b, dst, tab_ap, tab_b):
                if DKC == 1:
                    ew(lambda e: e.tensor_tensor(out=dst[:, :, :], in0=src_ap, in1=tab_ap, op=ALU.mult),
                       [src_b, tab_b], [dst.b])
                else:
                    sv = src_ap.rearrange("p (h c) t -> p h c t", c=DKC)
                    dv_ = dst[:, :, :].rearrange("p (h c) t -> p h c t", c=DKC)
                    for c in range(DKC):
                        ew(lambda e, c=c: e.tensor_tensor(out=dv_[:, :, c, :], in0=sv[:, :, c, :], in1=tab_ap, op=ALU.mult),
                           [src_b, tab_b], [dst.b])

            def gla_gates(z, xTt, ged):
                for kc in range(8):
                    S.op(S.pe, lambda kc=kc: nc.tensor.matmul(PS[1][0:16, 0:128], lhsT=wa1[:, z, kc, :], rhs=xT4[:, kc, xTt],
                                                              start=(kc == 0), stop=(kc == 7)),
                         reads=[wa1.b, xT4.b], writes=[PS[1].b], sig=(kc == 7))
                S.op(S.act, lambda: nc.scalar.copy(out=gtl[:, :], in_=PS[1][0:16, 0:128]), reads=[PS[1].b], writes=[gtl.b])
                for h in range(H):
                    S.op(S.pe, lambda h=h: nc.tensor.matmul(PS[7][:, h * 128:(h + 1) * 128], lhsT=wa2[:, z, h * 128:(h + 1) * 128],
                                                            rhs=gtl[:, :], start=True, stop=True),
                         reads=[wa2.b, gtl.b], writes=[PS[7].b], sig=(h == H - 1))
                bab = ba[:, z * 4:z * 4 + 4].unsqueeze(2).to_broadcast([128, H, 128])
                S.op(S.dve, lambda: nc.vector.tensor_tensor(out=gl[:, :, :], in0=PS[7][:, :].rearrange("p (h t) -> p h t", h=H),
                                                            in1=bab, op=ALU.add), reads=[PS[7].b, ba.b], writes=[gl.b])
                S.op(S.act, lambda: nc.scalar.activation(out=gl[:, :, :], in_=gl[:, :, :], func=AF.Exp, scale=-1.0),
                     reads=[gl.b], writes=[gl.b])
                S.op(S.act, lambda: nc.scalar.activation(out=gl[:, :, :], in_=gl[:, :, :], func=AF.Ln, bias=1.0, scale=1.0),
                     reads=[gl.b], writes=[gl.b])
                for h in range(H):
                    S.op(S.dve, lambda h=h: nc.vector.tensor_tensor_scan(out=gL[:, h, :], data0=ones32[:, :], data1=gl[:, h, :],
                                                                         initial=0.0, op0=ALU.mult, op1=ALU.add),
                         reads=[gl.b, ones32.b], writes=[gL.b])
                c0 = z * 4
                S.op(S.act, lambda: nc.scalar.activation(out=ged[:, c0:c0 + 4], in_=gL[:, :, 127], func=AF.Exp, scale=-1.0 / 16),
                     reads=[gL.b], writes=[ged.b])
                gb = ged[:, c0:c0 + 4].unsqueeze(2).to_broadcast([128, H, 128])
                if z == 0:
                    S.op(S.act, lambda: nc.scalar.activation(out=gE["a"][:, :, :], in_=gL[:, :, :], func=AF.Exp, scale=1.0 / 16),
                         reads=[gL.b], writes=[gE["a"].b])
                    S.op(S.act, lambda: nc.scalar.activation(out=gE["b"][:, :, :], in_=gL[:, :, :], func=AF.Exp, scale=-1.0 / 16),
                         reads=[gL.b], writes=[gE["b"].b])
                    ew(lambda e: e.tensor_tensor(out=gE["c"][:, :, :], in0=gE["a"][:, :, :], in1=gb, op=ALU.mult),
                       [gE["a"].b, ged.b], [gE["c"].b])
                    return {"EN": gE["a"], "EB": gE["b"], "ES": gE["c"]}
                S.op(S.act, lambda: nc.scalar.activation(out=ginv[:, :], in_=gL[:, :, 127], func=AF.Exp, scale=1.0 / 16),
                     reads=[gL.b], writes=[ginv.b])
                S.op(S.dve, lambda: nc.vector.tensor_tensor(out=gL[:, :, :], in0=gl[:, :, :], in1=gL[:, :, :], op=ALU.subtract),
                     reads=[gl.b, gL.b], writes=[gL.b])
                S.op(S.act, lambda: nc.scalar.activation(out=gE["c"][:, :, :], in_=gL[:, :, :], func=AF.Exp, scale=1.0 / 16),
                     reads=[gL.b], writes=[gE["c"].b])
                res = {"ES": gE["c"]}
                if mode == 2:
                    S.op(S.act, lambda: nc.scalar.activation(out=gE["b"][:, :, :], in_=gL[:, :, :], func=AF.Exp, scale=-1.0 / 16),
                         reads=[gL.b], writes=[gE["b"].b])
                    ib = ginv[:, :].unsqueeze(2).to_broadcast([128, H, 128])
                    ew(lambda e: e.tensor_tensor(out=gE["a"][:, :, :], in0=gE["c"][:, :, :], in1=ib, op=ALU.mult),
                       [gE["c"].b, ginv.b], [gE["a"].b])
                    ew(lambda e: e.tensor_tensor(out=gE["b"][:, :, :], in0=gE["b"][:, :, :], in1=gb, op=ALU.mult),
                       [gE["b"].b, ged.b], [gE["b"].b])
                    res["EN"] = gE["a"]
                    res["EB"] = gE["b"]
                return res

            def tab(z, nm, gt):
                if ret:
                    vi = z * 3 + {"EB": 0, "EN": 1, "ES": 2}[nm]
                    return tabs[:, vi, :, :], tabs.b
                return gt[nm][:, :, :], gt[nm].b

            def kst_transpose(srcv, dstk, dec=None):
                trv = PS[6].t[:, :].bitcast(BF16)
                for hc in range(NQ):
                    S.op(S.pe, lambda hc=hc: nc.tensor.transpose(trv[:, hc * 128:(hc + 1) * 128], srcv[:, hc, :], ident[:, :]),
                         reads=[srcv.b, ident.b], writes=[PS[6].b], sig=(hc == NQ - 1))
                if dec is None:
                    S.op(S.act, lambda: nc.scalar.copy(out=dstk[:, :], in_=trv[:, 0:QT]), reads=[PS[6].b], writes=[dstk.b])
                else:
                    for h in range(H):
                        S.op(S.act, lambda h=h: nc.scalar.activation(
                            out=dstk[:, h * DKC * 128:(h + 1) * DKC * 128], in_=trv[:, h * DKC * 128:(h + 1) * DKC * 128],
                            func=AF.Copy, scale=decK[:, dec * 4 + h:dec * 4 + h + 1]),
                            reads=[PS[6].b, decK.b], writes=[dstk.b])

            def prep(oi):
                n = order[oi]
                b2 = oi % 2
                csl = slice((n % 4) * 128, (n % 4 + 1) * 128)
                if oi % 4 == 0:
                    for j in range(oi, min(oi + 4, N)):
                        x_transpose(C, xbf[j % NXB], PS[6], xT4, (order[j] % 4) * 128, ident)
                    for j in range(oi + NXB, min(oi + NXB + 4, N)):
                        x_fetch(C, xsrc, order[j] * 128, xts[j % 2], xbf[j % NXB])
                    project(KC0, rawK)
                    if mode == 2:
                        project(QC0, rawQ)
                if ret:
                    for a_ in range(2):
                        S.dma(S.sp, cst[b2][:, a_, :, :],
                              cs_d[a_, :, n * 128:(n + 1) * 128].unsqueeze(1).to_broadcast([128, H, 128]),
                              reads=[cs_d.b], writes=[cst[b2].b], slot=cst[b2].b)
                if mode == 1:
                    for v4 in range(VT // 512):
                        ps_ = PS[v4 % 2]
                        for kc in range(8):
                            S.op(S.pe, lambda kc=kc, v4=v4, ps_=ps_: nc.tensor.matmul(
                                ps_[:, :], lhsT=xT4[:, kc, csl], rhs=wA[:, kc, QT + v4 * 512:QT + (v4 + 1) * 512],
                                start=(kc == 0), stop=(kc == 7)), reads=[wA.b, xT4.b], writes=[ps_.b], sig=(kc == 7))
                        S.op(S.act, lambda v4=v4, ps_=ps_: nc.scalar.copy(out=vtm[b2][:, v4 * 512:(v4 + 1) * 512], in_=ps_[:, :]),
                             reads=[ps_.b], writes=[vtm[b2].b])
                    S.dma(S.act, V_d[n, :, :], vtm[b2][:, :], reads=[vtm[b2].b], writes=[V_d.b], slot=vtm[b2].b)
                else:
                    S.dma(S.sp, vtm[b2][:, :], V_d[n, :, :], reads=[V_d.b], writes=[vtm[b2].b], slot=vtm[b2].b)
                    S.dma(S.sp, rbf[b2][:, :, :], R_d[n, :, :, :], reads=[R_d.b], writes=[rbf[b2].b], slot=rbf[b2].b)
                if ret:
                    rotary(rawK, csl, rotk, cst[b2], out=var["Kp"][b2])
                    if mode == 2:
                        rotary(rawQ, csl, rotq, cst[b2])
                        variant(rotq[:, :, :], rotq.b, var["Qf"][b2], *tab(0, "EB", None))
                        variant(rotq[:, :, :], rotq.b, var["Qb"][b2], *tab(1, "EB", None))
                    return
                ks_ap, ks_b = rawK[:, :, csl], rawK.b
                ged = None if ret else gedge[b2]
                if mode == 1:
                    gt = None if ret else gla_gates(1, csl, ged)
                    variant(ks_ap, ks_b, var["KSb"][b2], *tab(1, "ES", gt))
                    return
                gt = None if ret else gla_gates(0, csl, ged)
                variant(ks_ap, ks_b, var["Kf"][b2], *tab(0, "EN", gt))
                variant(ks_ap, ks_b, var["KSf"][b2], *tab(0, "ES", gt))
                if ret:
                    variant(ks_ap, ks_b, var["Kb"][b2], *tab(1, "EN", None))
                    rotary(rawQ, csl, rotq, cst[b2])
                    variant(rotq[:, :, :], rotq.b, var["Qf"][b2], *tab(0, "EB", None))
                    variant(rotq[:, :, :], rotq.b, var["Qb"][b2], *tab(1, "EB", None))
                else:
                    S.op(S.act, lambda: nc.scalar.mul(out=rotq[:, :, :], in_=rawQ[:, :, csl], mul=128.0 ** -0.5),
                         reads=[rawQ.b], writes=[rotq.b])
                    variant(rotq[:, :, :], rotq.b, var["Qf"][b2], *tab(0, "EB", gt))
                    gt1 = gla_gates(1, csl, ged)
                    variant(ks_ap, ks_b, var["Kb"][b2], *tab(1, "EN", gt1))
                    variant(rotq[:, :, :], rotq.b, var["Qb"][b2], *tab(1, "EB", gt1))

            def state_update(z, oi, heads=None):
                b2 = oi % 2
                banks = [PS[7], PS[0], PS[1]] if mode == 2 else [PS[7], PS[4], PS[5]]
                for h in (range(H) if heads is None else heads):
                    for c in range(DKC):
                        hc = h * DKC + c
                        ps_ = banks[hc % 3]
                        S.op(S.pe, lambda hc=hc, h=h, ps_=ps_: nc.tensor.matmul(
                            ps_[:, 0:DV], lhsT=kstm[b2][:, hc * 128:(hc + 1) * 128], rhs=vtm[b2][:, h * DV:(h + 1) * DV],
                            start=True, stop=True), reads=[kstm[b2].b, vtm[b2].b], writes=[ps_.b])
                        if ret:
                            ea, eb = edge[:, z * 4 + h:z * 4 + h + 1], edge.b
                        else:
                            ea, eb = gedge[b2][:, z * 4 + h:z * 4 + h + 1], gedge[b2].b
                        S.op(S.dve, lambda hc=hc, ps_=ps_, ea=ea: nc.vector.scalar_tensor_tensor(
                            out=st32[:, hc, :], in0=st32[:, hc, :], scalar=ea, in1=ps_[:, 0:DV], op0=ALU.mult, op1=ALU.add),
                            reads=[st32.b, ps_.b, eb], writes=[st32.b])
                        sdst = stbf[(oi + 1) % 2]
                        S.op(S.act, lambda hc=hc, sdst=sdst: nc.scalar.copy(out=sdst[:, hc, :], in_=st32[:, hc, :]),
                             reads=[st32.b], writes=[sdst.b])

            def engine(oi):
                n = order[oi]
                b2 = oi % 2
                if mode == 1:
                    if ret:
                        kst_transpose(var["Kp"][b2], kstm[b2], dec=1)
                    else:
                        kst_transpose(var["KSb"][b2], kstm[b2])
                    S.dma(S.act, R_d[n, :, :, :], stbf[b2][:, :, :], reads=[stbf[b2].b], writes=[R_d.b], slot=stbf[b2].b)
                    state_update(1, oi)
                    return
                if ret:
                    kst_transpose(var["Kp"][b2], kstm[b2], dec=0)
                else:
                    kst_transpose(var["KSf"][b2], kstm[b2])
                sold = stbf[b2]
                rb = rbf[b2]
                ys = yst[0]
                vt = vtm[b2]
                for h in range(H):
                    sc_ps = PS[2 + h % 2]
                    for (z, kn, qn_) in (((0, "Kp", "Qf"),) if ret else ((0, "Kf", "Qf"), (1, "Kb", "Qb"))):
                        for c in range(DKC):
                            hc = h * DKC + c
                            S.op(S.pe, lambda z=z, kn=kn, qn_=qn_, hc=hc, c=c, sc_ps=sc_ps: nc.tensor.matmul(
                                sc_ps[:, z * 128:(z + 1) * 128], lhsT=var[kn][b2][:, hc, :], rhs=var[qn_][b2][:, hc, :],
                                start=(c == 0), stop=(c == DKC - 1)), reads=[var[kn][b2].b, var[qn_][b2].b], writes=[sc_ps.b],
                                sig=((z == 1 or ret) and c == DKC - 1))
                    mk_ = msk[h % 2]
                    y_ps = PS[4 + h % 2]
                    mms = []
                    if ret:
                        S.op(S.dve, lambda mk_=mk_, sc_ps=sc_ps, h=h: nc.vector.tensor_tensor(
                            out=mk_[:, 0:128], in0=sc_ps[:, 0:128], in1=Dp[:, h, :], op=ALU.mult),
                            reads=[sc_ps.b, Dp.b], writes=[mk_.b])
                        mms.append((mk_[:, 0:128], vt[:, h * DV:(h + 1) * DV], [mk_.b, vt.b]))
                    else:
                        S.op(S.dve, lambda mk_=mk_, sc_ps=sc_ps: nc.vector.tensor_tensor(out=mk_[:, :], in0=sc_ps[:, 0:256], in1=mask[:, :],
                                                                                         op=ALU.mult),
                             reads=[sc_ps.b, mask.b], writes=[mk_.b])
                        mms.append((mk_[:, 0:128], vt[:, h * DV:(h + 1) * DV], [mk_.b, vt.b]))
                        mms.append((mk_[:, 128:256], vt[:, h * DV:(h + 1) * DV], [mk_.b, vt.b]))
                    for c in range(DKC):
                        hc = h * DKC + c
                        mms.append((var["Qb"][b2][:, hc, :], rb[:, hc, :], [var["Qb"][b2].b, rb.b]))
                    for c in range(DKC):
                        hc = h * DKC + c
                        mms.append((var["Qf"][b2][:, hc, :], sold[:, hc, :], [var["Qf"][b2].b, sold.b]))
                    for mi, (l_, r_, rd) in enumerate(mms):
                        S.op(S.pe, lambda l_=l_, r_=r_, mi=mi, y_ps=y_ps: nc.tensor.matmul(
                            y_ps[:, 0:DV], lhsT=l_, rhs=r_, start=(mi == 0), stop=(mi == len(mms) - 1)),
                            reads=rd, writes=[y_ps.b], sig=(mi == len(mms) - 1))
                    S.op(S.act, lambda h=h, y_ps=y_ps, ys=ys: nc.scalar.copy(out=ys[:, h * DV:(h + 1) * DV], in_=y_ps[:, 0:DV]),
                         reads=[y_ps.b], writes=[ys.b])
                    state_update(0, oi, [h])
                S.dma(S.act, y_d[n * 128:(n + 1) * 128, :], ys[:, :], reads=[ys.b], writes=[y_d.b], slot=ys.b)

            prep(0)
            for oi in range(N):
                if oi + 1 < N:
                    prep(oi + 1)
                engine(oi)
            C.end(ph)

        with ExitStack() as ph:
            phase12(ph, 1)
        with ExitStack() as ph:
            phase12(ph, 2)
        C.end(st)
    if hook is not None:
        hook("alloc")
    with ExitStack() as ph:
        wg = C.sb(ph, "wg", [128, 8, VT], BF16)
        load_w1(C, wg, wg[:, :, :], w_in[0, :, GOFF:GOFF + VT])
        NF = VT // 128
        wo = C.sb(ph, "wo", [128, NF, D], BF16)
        load_w1(C, wo, wo[:, :, :], w_o[0])
        gbc, bbc = load_ln_consts(C, ph, W, li, 0)
        if ret:
            nw = C.sb(ph, "nw", [128, VT], F32)
            nb = C.sb(ph, "nb", [128, VT], F32)
            S.dma(S.sp, nw[:, :], W["ret_gn_w"][0, :].partition_broadcast(128), reads=[], writes=[nw.b], slot=nw.b)
            S.dma(S.sp, nb[:, :], W["ret_gn_b"][0, :].partition_broadcast(128), reads=[], writes=[nb.b], slot=nb.b)
        else:
            nw = C.sb(ph, "nw", [128, DV], F32)
            S.dma(S.sp, nw[:, :], W["gla_norm"][0, :].partition_broadcast(128), reads=[], writes=[nw.b], slot=nw.b)
        xts = [None, None]
        xbf = [C.sb(ph, "xbf", [128, D], BF16) for _ in range(2)]
        xT = [C.sb(ph, "xT", [128, 8, 128], BF16) for _ in range(2)]
        yl = [C.sb(ph, "yl", [128, VT], F32) for _ in range(2)]
        gs = [C.sb(ph, "gs", [128, VT], F32) for _ in range(2)]
        gy = [C.sb(ph, "gy", [128, VT], BF16) for _ in range(2)]
        gyT = [C.sb(ph, "gyT", [128, NF, 128], BF16) for _ in range(2)]
        nst = [C.sb(ph, "nst", [128, 32], F32) for _ in range(2)]
        junk = C.sb(ph, "junk", [128, DV], F32)
        xrs = [C.sb(ph, "xr", [128, D], F32) for _ in range(2)]
        z = [C.sb(ph, "z", [128, D], F32) for _ in range(2)]
        stt = [C.sb(ph, "stt", [128, 16], F32) for _ in range(2)]
        for i in range(min(2, N)):
            x_fetch(C, xsrc, i * 128, xts[i], xbf[i])

        def prep3(n):
            b2 = n % 2
            xTt = xT[b2]
            x_transpose(C, xbf[b2], PS[6], xTt, 0, ident)
            if n + 2 < N:
                x_fetch(C, xsrc, (n + 2) * 128, xts[b2], xbf[b2])
            y = yl[b2]
            S.dma(S.sp, y[:, :], y_d[n * 128:(n + 1) * 128, :], reads=[y_d.b], writes=[y.b], slot=y.b)
            for g4 in range(VT // 512):
                ps_ = PS[g4 % 2]
                for kc in range(8):
                    S.op(S.pe, lambda kc=kc, g4=g4, ps_=ps_: nc.tensor.matmul(
                        ps_[:, :], lhsT=xTt[:, kc, :], rhs=wg[:, kc, g4 * 512:(g4 + 1) * 512],
                        start=(kc == 0), stop=(kc == 7)), reads=[wg.b, xTt.b], writes=[ps_.b], sig=(kc == 7))
                S.op(S.act, lambda g4=g4, ps_=ps_: nc.scalar.activation(out=gs[b2][:, g4 * 512:(g4 + 1) * 512], in_=ps_[:, :], func=AF.Silu),
                     reads=[ps_.b], writes=[gs[b2].b])
            ns = nst[b2]
            for h in range(H):
                ysl = y[:, h * DV:(h + 1) * DV]
                if ret:
                    S.op(S.dve, lambda h=h, ysl=ysl: nc.vector.bn_stats(out=ns[:, h * 6:(h + 1) * 6], in_=ysl),
                         reads=[y.b], writes=[ns.b])
                    S.op(S.dve, lambda h=h: nc.vector.bn_aggr(out=ns[:, 24 + h * 2:26 + h * 2], in_=ns[:, h * 6:(h + 1) * 6]),
                         reads=[ns.b], writes=[ns.b])
                else:
                    S.op(S.act, lambda h=h, ysl=ysl: nc.scalar.activation(out=junk[:, :], in_=ysl, func=AF.Square,
                                                                          accum_out=ns[:, 24 + h * 2 + 1:24 + h * 2 + 2]),
                         reads=[y.b], writes=[ns.b, junk.b])
            vv_ = ns[:, 24:32].rearrange("p (h t) -> p h t", t=2)
            if ret:
                S.op(S.act, lambda: nc.scalar.activation(out=vv_[:, :, 1], in_=vv_[:, :, 1], func=AF.Sqrt, bias=LN_EPS, scale=1.0),
                     reads=[ns.b], writes=[ns.b])
            else:
                S.op(S.act, lambda: nc.scalar.activation(out=vv_[:, :, 1], in_=vv_[:, :, 1], func=AF.Sqrt, bias=RMS_EPS, scale=1.0 / DV),
                     reads=[ns.b], writes=[ns.b])
            S.op(S.dve, lambda: nc.vector.reciprocal(out=vv_[:, :, 1], in_=vv_[:, :, 1]), reads=[ns.b], writes=[ns.b])
            yv = y[:, :].rearrange("p (h e) -> p h e", h=H)
            if ret:
                S.op(S.dve, lambda: nc.vector.scalar_tensor_tensor(out=vv_[:, :, 0], in0=vv_[:, :, 0], scalar=-1.0, in1=vv_[:, :, 1],
                                                                   op0=ALU.mult, op1=ALU.mult), reads=[ns.b], writes=[ns.b])
                for h in range(H):
                    ysl = y[:, h * DV:(h + 1) * DV]
                    S.op(S.act, lambda h=h, ysl=ysl: nc.scalar.activation(out=ysl, in_=ysl, func=AF.Identity,
                                                                          bias=ns[:, 24 + 2 * h:25 + 2 * h], scale=ns[:, 25 + 2 * h:26 + 2 * h]),
                         reads=[y.b, ns.b], writes=[y.b])
                S.op(S.pool, lambda: nc.gpsimd.tensor_tensor(out=y[:, 0:VT // 2], in0=y[:, 0:VT // 2], in1=nw[:, 0:VT // 2], op=ALU.mult),
                     reads=[y.b, nw.b], writes=[y.b])
                S.op(S.dve, lambda: nc.vector.tensor_tensor(out=y[:, VT // 2:VT], in0=y[:, VT // 2:VT], in1=nw[:, VT // 2:VT], op=ALU.mult),
                     reads=[y.b, nw.b], writes=[y.b])
                S.op(S.dve, lambda: nc.vector.tensor_tensor(out=y[:, :], in0=y[:, :], in1=nb[:, :], op=ALU.add),
                     reads=[y.b, nb.b], writes=[y.b])
            else:
                for h in range(H):
                    ysl = y[:, h * DV:(h + 1) * DV]
                    S.op(S.act, lambda h=h, ysl=ysl: nc.scalar.activation(out=ysl, in_=ysl, func=AF.Copy,
                                                                          scale=ns[:, 25 + 2 * h:26 + 2 * h]),
                         reads=[y.b, ns.b], writes=[y.b])
                nwb = nw[:, :].unsqueeze(1).to_broadcast([128, H, DV])
                S.op(S.pool, lambda: nc.gpsimd.tensor_tensor(out=yv, in0=yv, in1=nwb, op=ALU.mult), reads=[y.b, nw.b], writes=[y.b])
            S.op(S.dve, lambda: nc.vector.tensor_tensor(out=gy[b2][:, :], in0=y[:, :], in1=gs[b2][:, :], op=ALU.mult),
                 reads=[y.b, gs[b2].b], writes=[gy[b2].b])

        def engine3(n):
            b2 = n % 2
            for r8 in range(0, NF, 8):
                trv = PS[6].t[:, :].bitcast(BF16)
                for fc in range(r8, r8 + 8):
                    S.op(S.pe, lambda fc=fc, r8=r8: nc.tensor.transpose(trv[:, (fc - r8) * 128:(fc - r8 + 1) * 128],
                                                                        gy[b2][:, fc * 128:(fc + 1) * 128], ident[:, :]),
                         reads=[gy[b2].b, ident.b], writes=[PS[6].b], sig=(fc == r8 + 7))
                S.op(S.act, lambda r8=r8: nc.scalar.copy(out=gyT[b2][:, r8:r8 + 8, :], in_=trv.rearrange("p (k j) -> p k j", k=8)),
                     reads=[PS[6].b], writes=[gyT[b2].b])
            o_ps = [PS[2 + b2 * 2], PS[3 + b2 * 2]]
            for hf in range(2):
                for fc in range(NF):
                    S.op(S.pe, lambda hf=hf, fc=fc, o_ps=o_ps: nc.tensor.matmul(
                        o_ps[hf][:, :], lhsT=gyT[b2][:, fc, :], rhs=wo[:, fc, hf * 512:(hf + 1) * 512],
                        start=(fc == 0), stop=(fc == NF - 1)), reads=[gyT[b2].b, wo.b], writes=[o_ps[hf].b], sig=(fc == NF - 1))
            ln_epilogue(C, o_ps, xsrc, xrs[b2], gbc, bbc, z[b2], stt[b2], xdst, n * 128)

        prep3(0)
        if hook is not None:
            hook("load")
        for n in range(N):
            if n + 1 < N:
                prep3(n + 1)
            engine3(n)
        C.end(ph)


WSPECS = {
    "ln_g": ([DEPTH, 2, D], F32), "ln_b": ([DEPTH, 2, D], F32),
    "ffn_w_in": ([DEPTH, D, 2 * FH], F32), "ffn_w_out": ([DEPTH, FH, D], F32),
    "mla_w_down": ([2, D, 832], F32), "mla_q_norm": ([2, 512], F32), "mla_w_uq": ([2, 512, 1536], F32),
    "mla_kv_norm": ([2, 256], F32), "mla_w_ukv": ([2, 256, 2048], F32), "mla_w_o": ([2, 1024, D], F32),
    "ret_w_in": ([1, D, 6144], F32), "ret_decay_logit": ([1, 2, 4], F32), "ret_gn_w": ([1, 2048], F32),
    "ret_gn_b": ([1, 2048], F32), "ret_w_o": ([1, 2048, D], F32),
    "gla_w_in": ([1, D, 3072], F32), "gla_w_a1": ([1, 2, D, 16], F32), "gla_w_a2": ([1, 2, 16, 512], F32),
    "gla_b_a": ([1, 2, 512], F32), "gla_norm": ([1, 256], F32), "gla_w_o": ([1, 1024, D], F32),
}


def const_inputs():
    ident = np.eye(128, dtype=np.float32).astype(ml_dtypes.bfloat16)
    f64 = (10000.0 ** (-np.arange(0, 64, 2, dtype=np.float32) / np.float32(64))).astype(np.float32)
    f256 = (10000.0 ** (-np.arange(0, 256, 2, dtype=np.float32) / np.float32(256))).astype(np.float32)
    ii = np.arange(128, dtype=np.float32)
    io = np.stack([ii + 1, 127 - ii, 128 - ii, ii], 0)[None].repeat(128, 0).astype(np.float32)
    jj = np.arange(128)[:, None]
    mf = (ii[None, :] >= jj).astype(np.float32)
    mb = (jj > ii[None, :]).astype(np.float32)
    mask = np.concatenate([mf, mb], 1).astype(np.float32)
    dij = (ii[None, :] - jj).astype(np.float32)
    pcol = np.stack([127 - ii, ii], 1).astype(np.float32)
    return {"c_dij": dij, "c_pcol": pcol, "c_io": io, "c_mask": mask, "c_ident": ident, "c_invf64": np.concatenate([f64, f64])[:, None].astype(np.float32),
            "c_invf128": f256[:, None].astype(np.float32)}


def build_program(SEQ, plan):
    nc = bass.Bass("TRN2", target_bir_lowering=False)
    with ExitStack() as stack:
        C = Ctx(nc, stack)
        S = C.S
        x_in = C.dram("x", [SEQ, D], F32, kind="ExternalInput")
        pos_in = C.dram("positions", [SEQ], I32, kind="ExternalInput")
        W = {}
        for k, (shape, dt) in WSPECS.items():
            W[k] = nc.dram_tensor(k, shape, dt, kind="ExternalInput")
        ident_d = nc.dram_tensor("c_ident", [128, 128], BF16, kind="ExternalInput")
        CONST = {"pos": pos_in.t, "invf64": nc.dram_tensor("c_invf64", [64, 1], F32, kind="ExternalInput"),
                 "invf128": nc.dram_tensor("c_invf128", [128, 1], F32, kind="ExternalInput")}
        CONST["io"] = nc.dram_tensor("c_io", [128, 4, 128], F32, kind="ExternalInput")
        CONST["mask"] = nc.dram_tensor("c_mask", [128, 256], F32, kind="ExternalInput")
        CONST["dij"] = nc.dram_tensor("c_dij", [128, 128], F32, kind="ExternalInput")
        CONST["pcol"] = nc.dram_tensor("c_pcol", [128, 2], F32, kind="ExternalInput")
        SCR = {"oT": C.dram("oT_scr", [SEQ // 128, 128, 8, 128], BF16),
               "rope64": C.dram("rope64_scr", [2, 64, SEQ], F32)}
        kinds = set(k for k, _ in plan)
        if "ret" in kinds:
            SCR_RET = {"y": C.dram("y_ret", [SEQ, 2048], F32), "R": C.dram("R_ret", [SEQ // 128, 128, 8, 512], BF16),
                       "cs": C.dram("cs_ret", [2, 128, SEQ], F32), "V": C.dram("V_ret", [SEQ // 128, 128, 2048], BF16)}
        if "gla" in kinds:
            SCR_GLA = {"y": C.dram("y_gla", [SEQ, 1024], F32), "R": C.dram("R_gla", [SEQ // 128, 128, 4, 256], BF16),
                       "cs": None, "V": C.dram("V_gla", [SEQ // 128, 128, 1024], BF16)}
        out = C.dram("out", [SEQ, D], F32, kind="ExternalOutput")
        xa = C.dram("xa", [SEQ, D], F32)
        xb = C.dram("xb", [SEQ, D], F32)
        ident = C.sb(stack, "ident", [128, 128], BF16)
        S.dma(S.sp, ident[:, :], ident_d[:, :], reads=[], writes=[ident.b], slot=ident.b)
        PS = [C.ps(stack, "ps%d" % i, [128, 512], F32) for i in range(8)]
        src = x_in
        scratch = [xa, xb]
        pre = None
        fst = None
        for i, (kind, li) in enumerate(plan):
            dst = out if i == len(plan) - 1 else scratch[i % 2]
            hook = None
            if kind != "ffn" and i + 1 < len(plan) and plan[i + 1][0] == "ffn":
                which = {"mla": ("w1", "w2"), "ret": ("w2",), "gla": ("w1",)}[kind]
                fst = stack.enter_context(ExitStack())
                box = {}

                def hook(stage, which=which, fli=plan[i + 1][1], fst=fst, box=box):
                    if stage == "alloc":
                        box["pre"], box["new"] = ffn_alloc(C, fst, which)
                    else:
                        ffn_issue(C, W, fli, box["pre"], box["new"])
            if kind == "ffn":
                ffn_sublayer(C, W, li, src, dst, SEQ, PS, ident, pre)
                if fst is not None:
                    C.end(fst)
                    fst.close()
                    fst = None
                pre = None
            elif kind == "mla":
                mla_sublayer(C, W, li, src, dst, SEQ, PS, ident, CONST, SCR, hook)
            elif kind == "ret":
                lin_sublayer(C, W, li, src, dst, SEQ, PS, ident, CONST, SCR_RET, "ret", hook)
            elif kind == "gla":
                lin_sublayer(C, W, li, src, dst, SEQ, PS, ident, CONST, SCR_GLA, "gla", hook)
            else:
                raise NotImplementedError(kind)
            if hook is not None:
                pre = box.get("pre")
            src = dst
        S.barrier()
    return nc


FULL_PLAN = [("mla", 0), ("ffn", 0), ("ret", 1), ("ffn", 1), ("gla", 2), ("ffn", 2), ("mla", 3), ("ffn", 3)]


def run(inputs, SEQ, plan, n_cores, trace=False):
    nc = build_program(SEQ, plan)
    consts = const_inputs()
    in_maps = []
    for c in range(n_cores):
        m = {"x": np.ascontiguousarray(inputs["x"][c]), "positions": np.ascontiguousarray(inputs["positions"][c])}
        for k in WSPECS:
            m[k] = np.ascontiguousarray(inputs[k])
        m.update(consts)
        in_maps.append(m)
    res = run_bass_kernel_spmd(nc, in_maps, core_ids=list(range(n_cores)), trace=trace)
    outs = np.stack([np.asarray(r["out"]) for r in res.results], axis=0)
    return outs, res


def kernel(**inputs):
    inputs = {k: np.asarray(v) for k, v in inputs.items()}
    outs, _ = run(inputs, 4096, FULL_PLAN, 8)
    return outs.astype(np.float32)
```

```python
import math
from contextlib import ExitStack

import numpy as np
import ml_dtypes
import concourse.bass as bass
import concourse.mybir as mybir
from concourse.bass_utils import run_bass_kernel_spmd

F32 = mybir.dt.float32
BF16 = mybir.dt.bfloat16
I32 = mybir.dt.int32
ALU = mybir.AluOpType
AF = mybir.ActivationFunctionType

D = 1024
DEPTH = 4
ALPHA = (2.0 * DEPTH) ** 0.25
LN_EPS = 1e-5
RMS_EPS = 1e-6
FH = 2816
SEM_LIMIT = 30000


class Buf:
    __slots__ = ("name", "w", "r", "dsem", "dcnt", "dgen", "uid")
    _n = [0]

    def __init__(self, name):
        Buf._n[0] += 1
        self.uid = Buf._n[0]
        self.name = name
        self.w = {}
        self.r = {}
        self.dsem = None
        self.dcnt = 0
        self.dgen = 0


class Tl:
    def __init__(self, t, b):
        self.t = t
        self.b = b

    def __getitem__(self, k):
        return self.t[k]


class Eng:
    def __init__(self, name, h):
        self.name = name
        self.h = h
        self.sem = None
        self.gen = 0
        self.cnt = 0
        self.waited = {}


class Sched:
    def __init__(self, nc, stack):
        self.nc = nc
        self.stack = stack
        self.pe = Eng("pe", nc.tensor)
        self.act = Eng("act", nc.scalar)
        self.dve = Eng("dve", nc.vector)
        self.pool = Eng("pool", nc.gpsimd)
        self.sp = Eng("sp", nc.sync)
        self.engs = [self.pe, self.act, self.dve, self.pool, self.sp]
        self.nsem = 0
        self.dma_bufs = []
        self.sem_pool = []
        for e in self.engs:
            e.sem = self._newsem(e.name)

    def _newsem(self, name):
        self.nsem += 1
        return self.stack.enter_context(self.nc.semaphore("s_%s_%d" % (name, self.nsem)))

    @staticmethod
    def _merge(deps, d, skip=None):
        for k, (sem, val) in d.items():
            if skip is not None and k[0] == skip:
                continue
            cur = deps.get(k)
            if cur is None or cur[1] < val:
                deps[k] = (sem, val)

    def _need(self, eng, deps):
        for k, (sem, val) in deps.items():
            if eng.waited.get(k, 0) < val:
                eng.h.wait_ge(sem, val)
                eng.waited[k] = val

    def op(self, eng, fn, reads=(), writes=(), sig=True):
        deps = {}
        for b in reads:
            self._merge(deps, b.w)
        for b in writes:
            self._merge(deps, b.w, skip=eng.name)
            self._merge(deps, b.r, skip=eng.name)
        self._need(eng, deps)
        if eng.cnt >= SEM_LIMIT:
            eng.sem = self._newsem(eng.name)
            eng.gen += 1
            eng.cnt = 0
        inst = fn()
        key = (eng.name, eng.gen)
        if sig:
            eng.cnt += 1
            inst.then_inc(eng.sem, 1)
            tick = (eng.sem, eng.cnt)
        else:
            tick = (eng.sem, eng.cnt + 1)
        for b in reads:
            b.r[key] = tick
        for b in writes:
            b.w[key] = tick
        return inst

    def dma(self, eng, out, in_, reads=(), writes=(), slot=None):
        deps = {}
        for b in reads:
            self._merge(deps, b.w)
        for b in writes:
            self._merge(deps, b.w)
            self._merge(deps, b.r)
        self._need(eng, deps)
        if slot.dsem is None or slot.dcnt + 16 > SEM_LIMIT:
            if slot.dsem is None:
                self.dma_bufs.append(slot)
            if slot.dsem is None and self.sem_pool and self.sem_pool[-1][1] + 16 <= SEM_LIMIT:
                slot.dsem, slot.dcnt = self.sem_pool.pop()
            else:
                slot.dsem = self._newsem("d")
                slot.dcnt = 0
            slot.dgen += 1
        slot.dcnt += 16
        eng.h.dma_start(out=out, in_=in_).then_inc(slot.dsem, 16)
        key = ("dma", slot.uid, slot.dgen)
        tick = (slot.dsem, slot.dcnt)
        for b in reads:
            b.r[key] = tick
        for b in writes:
            b.w[key] = tick

    def barrier(self, engs=None):
        deps = {}
        for e in self.engs:
            if e.cnt > 0:
                deps[(e.name, e.gen)] = (e.sem, e.cnt)
        for s in self.dma_bufs:
            if s.dcnt > 0:
                deps[("dma", s.uid, s.dgen)] = (s.dsem, s.dcnt)
        for e in (engs or self.engs):
            self._need(e, deps)

    def release(self, bufs):
        for b in bufs:
            if b.dsem is not None:
                self.sem_pool.append((b.dsem, b.dcnt))
                self.sem_pool.sort(key=lambda t: -t[1])
                b.dsem = None
                if b in self.dma_bufs:
                    self.dma_bufs.remove(b)


class Ctx:
    def __init__(self, nc, stack):
        self.nc = nc
        self.S = Sched(nc, stack)
        self.stack = stack
        self.n = 0
        self.live = {}

    def sb(self, st, name, shape, dtype):
        self.n += 1
        t = st.enter_context(self.nc.sbuf_tensor("%s_%d" % (name, self.n), list(shape), dtype))
        b = Buf(name)
        self.live.setdefault(id(st), []).append(b)
        return Tl(t, b)

    def end(self, st):
        self.S.barrier()
        self.S.release(self.live.pop(id(st), []))

    def ps(self, st, name, shape, dtype):
        self.n += 1
        t = st.enter_context(self.nc.psum_tensor("%s_%d" % (name, self.n), list(shape), dtype))
        return Tl(t, Buf(name))

    def dram(self, name, shape, dtype, kind="Internal"):
        t = self.nc.dram_tensor(name, list(shape), dtype, kind=kind)
        return Tl(t, Buf(name))


def x_fetch(C, xsrc, tok0, xt_tl, xbf_tl):
    S = C.S
    if xt_tl is None:
        S.dma(S.pool, xbf_tl[:, :], xsrc[tok0:tok0 + 128, :], reads=[xsrc.b], writes=[xbf_tl.b], slot=xbf_tl.b)
        return
    S.dma(S.sp, xt_tl[:, :], xsrc[tok0:tok0 + 128, :], reads=[xsrc.b], writes=[xt_tl.b], slot=xt_tl.b)
    S.op(S.pool, lambda: C.nc.gpsimd.tensor_copy(out=xbf_tl[:, :], in_=xt_tl[:, :]),
         reads=[xt_tl.b], writes=[xbf_tl.b])


def x_transpose(C, xbf_tl, ps_tr, xT_tl, col0, ident, nk=8):
    S = C.S
    trv = ps_tr.t[:, :].bitcast(BF16)
    for kc in range(nk):
        S.op(S.pe, lambda kc=kc: C.nc.tensor.transpose(trv[:, kc * 128:(kc + 1) * 128],
                                                       xbf_tl[:, kc * 128:(kc + 1) * 128], ident[:, :]),
             reads=[xbf_tl.b, ident.b], writes=[ps_tr.b], sig=(kc == nk - 1))
    S.op(S.act, lambda: C.nc.scalar.copy(out=xT_tl[:, 0:nk, col0:col0 + 128],
                                         in_=trv[:, 0:nk * 128].rearrange("p (k j) -> p k j", k=nk)),
         reads=[ps_tr.b], writes=[xT_tl.b])


def ln_epilogue(C, y_ps, xsrc, xr_tl, gbc, bbc, z_tl, st_tl, xdst, tok0, pool_free=False):
    S = C.S
    nc = C.nc
    S.dma(S.sp, xr_tl[:, :], xsrc[tok0:tok0 + 128, :], reads=[xsrc.b], writes=[xr_tl.b], slot=xr_tl.b)
    for h in range(2):
        S.op(S.dve, lambda h=h: nc.vector.scalar_tensor_tensor(
            out=z_tl[:, h * 512:(h + 1) * 512], in0=xr_tl[:, h * 512:(h + 1) * 512], scalar=ALPHA,
            in1=y_ps[h][:, :], op0=ALU.mult, op1=ALU.add),
            reads=[xr_tl.b, y_ps[h].b], writes=[z_tl.b])
    for h in range(2):
        S.op(S.dve, lambda h=h: nc.vector.bn_stats(out=st_tl[:, h * 6:(h + 1) * 6],
                                                   in_=z_tl[:, h * 512:(h + 1) * 512]),
             reads=[z_tl.b], writes=[st_tl.b])
    S.op(S.dve, lambda: nc.vector.bn_aggr(out=st_tl[:, 12:14], in_=st_tl[:, 0:12]),
         reads=[st_tl.b], writes=[st_tl.b])
    S.op(S.act, lambda: nc.scalar.activation(out=st_tl[:, 14:15], in_=st_tl[:, 13:14], func=AF.Sqrt,
                                             bias=LN_EPS, scale=1.0),
         reads=[st_tl.b], writes=[st_tl.b])
    S.op(S.dve, lambda: nc.vector.reciprocal(out=st_tl[:, 14:15], in_=st_tl[:, 14:15]),
         reads=[st_tl.b], writes=[st_tl.b])
    S.op(S.dve, lambda: nc.vector.scalar_tensor_tensor(out=st_tl[:, 15:16], in0=st_tl[:, 12:13], scalar=-1.0,
                                                       in1=st_tl[:, 14:15], op0=ALU.mult, op1=ALU.mult),
         reads=[st_tl.b], writes=[st_tl.b])
    S.op(S.act, lambda: nc.scalar.activation(out=z_tl[:, :], in_=z_tl[:, :], func=AF.Identity,
                                             bias=st_tl[:, 15:16], scale=st_tl[:, 14:15]),
         reads=[z_tl.b, st_tl.b], writes=[z_tl.b])
    S.op(S.dve, lambda: nc.vector.tensor_tensor(out=z_tl[:, :], in0=z_tl[:, :], in1=gbc[:, :], op=ALU.mult),
         reads=[z_tl.b, gbc.b], writes=[z_tl.b])
    if pool_free:
        S.op(S.dve, lambda: nc.vector.tensor_tensor(out=z_tl[:, :], in0=z_tl[:, :], in1=bbc[:, :], op=ALU.add),
             reads=[z_tl.b, bbc.b], writes=[z_tl.b])
        S.dma(S.act, xdst[tok0:tok0 + 128, :], z_tl[:, :], reads=[z_tl.b], writes=[xdst.b], slot=z_tl.b)
        return
    S.op(S.pool, lambda: nc.gpsimd.tensor_tensor(out=z_tl[:, :], in0=z_tl[:, :], in1=bbc[:, :], op=ALU.add),
         reads=[z_tl.b, bbc.b], writes=[z_tl.b])
    S.dma(S.pool, xdst[tok0:tok0 + 128, :], z_tl[:, :], reads=[z_tl.b], writes=[xdst.b], slot=z_tl.b)


def load_ln_consts(C, st, W, li, which):
    S = C.S
    gbc = C.sb(st, "gbc", [128, D], F32)
    bbc = C.sb(st, "bbc", [128, D], F32)
    S.dma(S.sp, gbc[:, :], W["ln_g"][li, which, :].partition_broadcast(128), reads=[], writes=[gbc.b], slot=gbc.b)
    S.dma(S.sp, bbc[:, :], W["ln_b"][li, which, :].partition_broadcast(128), reads=[], writes=[bbc.b], slot=bbc.b)
    return gbc, bbc


def ffn_alloc(C, st, which, have=None):
    res = dict(have or {})
    new = []
    if "w1" in which and "w1" not in res:
        res["w1"] = C.sb(st, "w1", [128, 8, 2 * FH], BF16)
        new.append("w1")
    if "w2" in which and "w2" not in res:
        res["w2"] = C.sb(st, "w2", [128, 22, D], BF16)
        new.append("w2")
    return res, new


def ffn_issue(C, W, li, tiles, names):
    S = C.S
    if "w1" in names:
        w1 = tiles["w1"]
        for k2 in range(2):
            S.dma(S.pool, w1[:, k2 * 4:(k2 + 1) * 4, :],
                  W["ffn_w_in"][li, k2 * 512:(k2 + 1) * 512, :].rearrange("(kc p) n -> p kc n", p=128),
                  reads=[], writes=[w1.b], slot=w1.b)
    if "w2" in names:
        w2 = tiles["w2"]
        S.dma(S.pool, w2[:, :, :], W["ffn_w_out"][li, :, :].rearrange("(fc p) n -> p fc n", p=128),
              reads=[], writes=[w2.b], slot=w2.b)


def ffn_sublayer(C, W, li, xsrc, xdst, SEQ, PS, ident, pre=None):
    S = C.S
    nc = C.nc
    TT = 512
    with ExitStack() as st:
        wts, newn = ffn_alloc(C, st, ("w1", "w2"), pre)
        ffn_issue(C, W, li, wts, newn)
        w1, w2 = wts["w1"], wts["w2"]
        gbc, bbc = load_ln_consts(C, st, W, li, 1)
        xts = [C.sb(st, "xt", [128, D], F32) for _ in range(2)]
        xrs = [C.sb(st, "xr", [128, D], F32) for _ in range(2)]
        xbf = [C.sb(st, "xbf", [128, D], BF16) for _ in range(4)]
        xT = [C.sb(st, "xT", [128, 8, TT], BF16) for _ in range(1)]
        aT = C.sb(st, "aT", [128, 22, TT], BF16)
        sg = [C.sb(st, "sg", [128, TT], F32) for _ in range(2)]
        z = [C.sb(st, "z", [128, D], F32) for _ in range(2)]
        stt = [C.sb(st, "stt", [128, 16], F32) for _ in range(2)]
        nt = SEQ // TT
        for s in range(4):
            x_fetch(C, xsrc, s * 128, xts[s % 2], xbf[s])
        for t in range(nt):
            xTt = xT[0]
            for s in range(4):
                x_transpose(C, xbf[s], PS[6], xTt, s * 128, ident)
            if t + 1 < nt:
                for s in range(4):
                    x_fetch(C, xsrc, (t + 1) * TT + s * 128, xts[s % 2], xbf[s])
            for fc in range(22):
                g_ps = PS[(fc % 2) * 2]
                u_ps = PS[(fc % 2) * 2 + 1]
                for kc in range(8):
                    S.op(S.pe, lambda kc=kc, fc=fc, g_ps=g_ps: nc.tensor.matmul(
                        g_ps[:, :], lhsT=w1[:, kc, fc * 128:(fc + 1) * 128], rhs=xTt[:, kc, :],
                        start=(kc == 0), stop=(kc == 7)),
                        reads=[w1.b, xTt.b], writes=[g_ps.b], sig=(kc == 7))
                for kc in range(8):
                    S.op(S.pe, lambda kc=kc, fc=fc, u_ps=u_ps: nc.tensor.matmul(
                        u_ps[:, :], lhsT=w1[:, kc, FH + fc * 128:FH + (fc + 1) * 128], rhs=xTt[:, kc, :],
                        start=(kc == 0), stop=(kc == 7)),
                        reads=[w1.b, xTt.b], writes=[u_ps.b], sig=(kc == 7))
                sgt = sg[fc % 2]
                S.op(S.act, lambda g_ps=g_ps, sgt=sgt: nc.scalar.activation(out=sgt[:, :], in_=g_ps[:, :], func=AF.Silu),
                     reads=[g_ps.b], writes=[sgt.b])
                S.op(S.dve, lambda u_ps=u_ps, sgt=sgt, fc=fc: nc.vector.tensor_tensor(
                    out=aT[:, fc, :], in0=u_ps[:, :], in1=sgt[:, :], op=ALU.mult),
                    reads=[u_ps.b, sgt.b], writes=[aT.b])
            for s in range(4):
                o_ps = [PS[4], PS[5]] if s % 2 == 0 else [PS[7], PS[6]]
                for h in range(2):
                    for fc in range(22):
                        S.op(S.pe, lambda h=h, fc=fc, s=s, o_ps=o_ps: nc.tensor.matmul(
                            o_ps[h][:, :], lhsT=aT[:, fc, s * 128:(s + 1) * 128], rhs=w2[:, fc, h * 512:(h + 1) * 512],
                            start=(fc == 0), stop=(fc == 21)),
                            reads=[aT.b, w2.b], writes=[o_ps[h].b], sig=(fc == 21))
                ln_epilogue(C, o_ps, xsrc, xrs[s % 2], gbc, bbc, z[s % 2], stt[s % 2], xdst, t * TT + s * 128)
        C.end(st)


MAGIC = 12582912.0
CW1 = 6.28125
CW2 = 2 * math.pi - CW1
PI_SAFE = 3.1415925


def rope_tables(C, st, tst, pos_in, invf_d, npart, SEQ):
    S = C.S
    nc = C.nc
    cos = C.sb(st, "cos", [npart, SEQ], F32)
    sin = C.sb(st, "sin", [npart, SEQ], F32)
    pi_ = C.sb(tst, "posi", [npart, SEQ], I32)
    a = C.sb(tst, "ang", [npart, SEQ], F32)
    k = C.sb(tst, "kk", [npart, SEQ], F32)
    fr = C.sb(tst, "fr", [npart, 1], F32)
    S.dma(S.sp, pi_[:, :], pos_in[:].partition_broadcast(npart), reads=[], writes=[pi_.b], slot=pi_.b)
    S.dma(S.sp, fr[:, :], invf_d[:, :], reads=[], writes=[fr.b], slot=fr.b)
    S.op(S.dve, lambda: nc.vector.tensor_copy(out=a[:, :], in_=pi_[:, :]), reads=[pi_.b], writes=[a.b])
    S.op(S.dve, lambda: nc.vector.tensor_scalar(out=a[:, :], in0=a[:, :], scalar1=fr[:, 0:1], scalar2=None,
                                                op0=ALU.mult), reads=[a.b, fr.b], writes=[a.b])
    for (dst, shift) in ((sin, 0.0), (cos, 0.25)):
        S.op(S.dve, lambda shift=shift: nc.vector.tensor_scalar(
            out=k[:, :], in0=a[:, :], scalar1=1.0 / (2 * math.pi), scalar2=shift, op0=ALU.mult, op1=ALU.add),
            reads=[a.b], writes=[k.b])
        S.op(S.dve, lambda: nc.vector.tensor_scalar(out=k[:, :], in0=k[:, :], scalar1=MAGIC, scalar2=MAGIC,
                                                    op0=ALU.add, op1=ALU.subtract), reads=[k.b], writes=[k.b])
        S.op(S.dve, lambda dst=dst: nc.vector.scalar_tensor_tensor(
            out=dst[:, :], in0=k[:, :], scalar=-CW1, in1=a[:, :], op0=ALU.mult, op1=ALU.add),
            reads=[a.b, k.b], writes=[dst.b])
        S.op(S.dve, lambda dst=dst: nc.vector.scalar_tensor_tensor(
            out=dst[:, :], in0=k[:, :], scalar=-CW2, in1=dst[:, :], op0=ALU.mult, op1=ALU.add),
            reads=[dst.b, k.b], writes=[dst.b])
        S.op(S.dve, lambda dst=dst, shift=shift: nc.vector.tensor_scalar(
            out=dst[:, :], in0=dst[:, :], scalar1=shift * 2 * math.pi, scalar2=PI_SAFE, op0=ALU.add, op1=ALU.min),
            reads=[dst.b], writes=[dst.b])
        S.op(S.dve, lambda dst=dst: nc.vector.tensor_scalar(
            out=dst[:, :], in0=dst[:, :], scalar1=-PI_SAFE, scalar2=None, op0=ALU.max),
            reads=[dst.b], writes=[dst.b])
        S.op(S.act, lambda dst=dst: nc.scalar.activation(out=dst[:, :], in_=dst[:, :], func=AF.Sin),
             reads=[dst.b], writes=[dst.b])
    return cos, sin


def load_w1(C, dst_tl, dst_ap, src2d):
    C.S.dma(C.S.pool, dst_ap, src2d.rearrange("(kc p) n -> p kc n", p=128), reads=[], writes=[dst_tl.b], slot=dst_tl.b)


def load_w_bf16(C, dst_tl, dst_fn, src_fn, nchunks):
    S = C.S
    for kc in range(nchunks):
        S.dma(S.pool, dst_fn(kc), src_fn(kc), reads=[], writes=[dst_tl.b], slot=dst_tl.b)


def mla_sublayer(C, W, li, xsrc, xdst, SEQ, PS, ident, CONST, SCR, hook=None):
    S = C.S
    nc = C.nc
    j = li // 3
    TT = 512
    nt = SEQ // TT
    nkc = SEQ // 128
    SC = 192.0 ** -0.5
    oT_d = SCR["oT"]
    with ExitStack() as st:
        cqn = C.sb(st, "cqn", [128, 4, SEQ], BF16)
        ckvn = C.sb(st, "ckvn", [128, 2, SEQ], BF16)
        krot = C.sb(st, "krot", [128, SEQ], BF16)
        S.op(S.pool, lambda: nc.gpsimd.memset(krot[64:128, :], 0.0), writes=[krot.b])
        ones = C.sb(st, "ones", [128, 128], BF16)
        S.op(S.pool, lambda: nc.gpsimd.memset(ones[:, :], 1.0), writes=[ones.b])
        rc = SCR.get("rope64")
        if rc is not None and SCR.get("rope64_ready"):
            cos = C.sb(st, "cos", [64, SEQ], F32)
            sin = C.sb(st, "sin", [64, SEQ], F32)
            S.dma(S.sp, cos[:, :], rc[0, :, :], reads=[rc.b], writes=[cos.b], slot=cos.b)
            S.dma(S.sp, sin[:, :], rc[1, :, :], reads=[rc.b], writes=[sin.b], slot=sin.b)
        else:
            with ExitStack() as tst:
                cos, sin = rope_tables(C, st, tst, CONST["pos"], CONST["invf64"], 64, SEQ)
                C.end(tst)
            if rc is not None:
                S.dma(S.act, rc[0, :, :], cos[:, :], reads=[cos.b], writes=[rc.b], slot=cos.b)
                S.dma(S.act, rc[1, :, :], sin[:, :], reads=[sin.b], writes=[rc.b], slot=sin.b)
                SCR["rope64_ready"] = True
        with ExitStack() as pa:
            wd = C.sb(pa, "wd", [128, 8, 960], BF16)
            S.op(S.pool, lambda: nc.gpsimd.memset(wd[:, :, 896:960], 0.0), writes=[wd.b])
            load_w1(C, wd, wd[:, :, 0:832], W["mla_w_down"][j])
            S.op(S.act, lambda: nc.scalar.mul(out=wd[:, :, 832:864], in_=wd[:, :, 800:832], mul=-1.0),
                 reads=[wd.b], writes=[wd.b])
            S.op(S.act, lambda: nc.scalar.copy(out=wd[:, :, 864:896], in_=wd[:, :, 768:800]),
                 reads=[wd.b], writes=[wd.b])
            qg = C.sb(pa, "qg", [128, 4], F32)
            kvg = C.sb(pa, "kvg", [128, 2], F32)
            for c in range(4):
                S.dma(S.sp, qg[:, c:c + 1], W["mla_q_norm"][j, c * 128:(c + 1) * 128].rearrange("(p o) -> p o", o=1),
                      reads=[], writes=[qg.b], slot=qg.b)
            for c in range(2):
                S.dma(S.sp, kvg[:, c:c + 1], W["mla_kv_norm"][j, c * 128:(c + 1) * 128].rearrange("(p o) -> p o", o=1),
                      reads=[], writes=[kvg.b], slot=kvg.b)
            xts = [C.sb(pa, "xt", [128, D], F32) for _ in range(2)]
            xbf = [C.sb(pa, "xbf", [128, D], BF16) for _ in range(4)]
            xT = C.sb(pa, "xT", [128, 8, TT], BF16)
            raw = C.sb(pa, "raw", [128, 6, TT], F32)
            sq = [C.sb(pa, "sq", [128, TT], BF16) for _ in range(2)]
            rs = [C.sb(pa, "rs", [128, TT], F32) for _ in range(2)]
            t1 = C.sb(pa, "t1", [64, TT], F32)
            t2 = C.sb(pa, "t2", [64, TT], F32)
            for s in range(4):
                x_fetch(C, xsrc, s * 128, xts[s % 2], xbf[s])
            for t in range(nt):
                tsl = slice(t * TT, (t + 1) * TT)
                for s in range(4):
                    x_transpose(C, xbf[s], PS[6], xT, s * 128, ident)
                if t + 1 < nt:
                    for s in range(4):
                        x_fetch(C, xsrc, (t + 1) * TT + s * 128, xts[s % 2], xbf[s])
                pend_ss = []
                for oc in range(6):
                    d_ps = PS[oc % 2]
                    ss_ps = PS[2] if oc < 4 else PS[3]
                    for kc in range(8):
                        S.op(S.pe, lambda kc=kc, oc=oc, d_ps=d_ps: nc.tensor.matmul(
                            d_ps[:, :], lhsT=wd[:, kc, oc * 128:(oc + 1) * 128], rhs=xT[:, kc, :],
                            start=(kc == 0), stop=(kc == 7)), reads=[wd.b, xT.b], writes=[d_ps.b], sig=(kc == 7))
                    S.op(S.act, lambda oc=oc, d_ps=d_ps: nc.scalar.copy(out=raw[:, oc, :], in_=d_ps[:, :]),
                         reads=[d_ps.b], writes=[raw.b])
                    sqt = sq[oc % 2]
                    S.op(S.act, lambda d_ps=d_ps, sqt=sqt: nc.scalar.activation(out=sqt[:, :], in_=d_ps[:, :], func=AF.Square),
                         reads=[d_ps.b], writes=[sqt.b])
                    first = oc in (0, 4)
                    last = oc in (3, 5)
                    if pend_ss:
                        pend_ss.pop()()
                    pend_ss.append(lambda sqt=sqt, ss_ps=ss_ps, first=first, last=last: S.op(
                        S.pe, lambda: nc.tensor.matmul(ss_ps[:, :], lhsT=ones[:, :], rhs=sqt[:, :], start=first, stop=last),
                        reads=[ones.b, sqt.b], writes=[ss_ps.b], sig=True))
                pend_ss.pop()()
                for (ss_ps, rst, n) in ((PS[2], rs[0], 512.0), (PS[3], rs[1], 256.0)):
                    S.op(S.act, lambda ss_ps=ss_ps, rst=rst, n=n: nc.scalar.activation(
                        out=rst[:, :], in_=ss_ps[:, :], func=AF.Sqrt, bias=RMS_EPS, scale=1.0 / n),
                        reads=[ss_ps.b], writes=[rst.b])
                    S.op(S.dve, lambda rst=rst: nc.vector.reciprocal(out=rst[:, :], in_=rst[:, :]),
                         reads=[rst.b], writes=[rst.b])
                for c in range(4):
                    S.op(S.dve, lambda c=c: nc.vector.scalar_tensor_tensor(
                        out=cqn[:, c, tsl], in0=raw[:, c, :], scalar=qg[:, c:c + 1], in1=rs[0][:, :],
                        op0=ALU.mult, op1=ALU.mult), reads=[raw.b, qg.b, rs[0].b], writes=[cqn.b])
                for c in range(2):
                    S.op(S.dve, lambda c=c: nc.vector.scalar_tensor_tensor(
                        out=ckvn[:, c, tsl], in0=raw[:, 4 + c, :], scalar=kvg[:, c:c + 1], in1=rs[1][:, :],
                        op0=ALU.mult, op1=ALU.mult), reads=[raw.b, kvg.b, rs[1].b], writes=[ckvn.b])
                for (ps_, c0) in ((PS[4], 768), (PS[5], 832)):
                    for kc in range(8):
                        S.op(S.pe, lambda kc=kc, ps_=ps_, c0=c0: nc.tensor.matmul(
                            ps_[:, :], lhsT=wd[:, kc, c0:c0 + 128], rhs=xT[:, kc, :],
                            start=(kc == 0), stop=(kc == 7)), reads=[wd.b, xT.b], writes=[ps_.b], sig=(kc == 7))
                S.op(S.dve, lambda: nc.vector.tensor_tensor(out=t1[:, :], in0=PS[4][0:64, :], in1=cos[:, tsl], op=ALU.mult),
                     reads=[PS[4].b, cos.b], writes=[t1.b])
                S.op(S.dve, lambda: nc.vector.tensor_tensor(out=t2[:, :], in0=PS[5][0:64, :], in1=sin[:, tsl], op=ALU.mult),
                     reads=[PS[5].b, sin.b], writes=[t2.b])
                S.op(S.pool, lambda: nc.gpsimd.tensor_tensor(out=krot[0:64, tsl], in0=t1[:, :], in1=t2[:, :], op=ALU.add),
                     reads=[t1.b, t2.b], writes=[krot.b])
            C.end(pa)
        with ExitStack() as pb:
            wq = C.sb(pb, "wq", [128, 4, 2112], BF16)
            S.op(S.pool, lambda: nc.gpsimd.memset(wq[:, :, 2048:2112], 0.0), writes=[wq.b])
            wkv = C.sb(pb, "wkv", [128, 2, 2048], BF16)
            load_w1(C, wq, wq[:, :, 0:1536], W["mla_w_uq"][j])
            load_w1(C, wkv, wkv[:, :, :], W["mla_w_ukv"][j])
            for h in range(8):
                S.op(S.act, lambda h=h: nc.scalar.mul(out=wq[:, :, 1536 + h * 64:1536 + h * 64 + 32],
                                                      in_=wq[:, :, h * 192 + 160:h * 192 + 192], mul=-1.0),
                     reads=[wq.b], writes=[wq.b])
                S.op(S.act, lambda h=h: nc.scalar.copy(out=wq[:, :, 1536 + h * 64 + 32:1536 + h * 64 + 64],
                                                       in_=wq[:, :, h * 192 + 128:h * 192 + 160]),
                     reads=[wq.b], writes=[wq.b])
            kT = [C.sb(pb, "kT", [128, SEQ], BF16) for _ in range(2)]
            vv = [C.sb(pb, "vv", [128, nkc, 128], BF16) for _ in range(2)]
            qn = [C.sb(pb, "qn", [128, TT], BF16) for _ in range(2)]
            qr = [C.sb(pb, "qr", [128, TT], BF16) for _ in range(2)]
            for q_ in qr:
                S.op(S.pool, lambda q_=q_: nc.gpsimd.memset(q_[64:128, :], 0.0), writes=[q_.b])
            u1 = C.sb(pb, "u1", [64, TT], F32)
            u2 = C.sb(pb, "u2", [64, TT], F32)
            pT = [C.sb(pb, "pT", [128, TT], BF16) for _ in range(6)]
            rinv = C.sb(pb, "rinv", [128, TT], F32)
            on = [C.sb(pb, "on", [128, TT], BF16) for _ in range(2)]
            units = [(h, qt) for h in range(8) for qt in range(nt)]

            def prep(u):
                h, qt = units[u]
                if qt == 0:
                    kTh = kT[h % 2]
                    vh = vv[h % 2]
                    for t in range(nt):
                        ps_ = PS[6 + t % 2]
                        for c in range(2):
                            S.op(S.pe, lambda c=c, t=t, ps_=ps_: nc.tensor.matmul(
                                ps_[:, :], lhsT=wkv[:, c, h * 256:h * 256 + 128], rhs=ckvn[:, c, t * TT:(t + 1) * TT],
                                start=(c == 0), stop=(c == 1)), reads=[wkv.b, ckvn.b], writes=[ps_.b], sig=(c == 1))
                        S.op(S.act, lambda t=t, ps_=ps_: nc.scalar.copy(out=kTh[:, t * TT:(t + 1) * TT], in_=ps_[:, :]),
                             reads=[ps_.b], writes=[kTh.b])
                    for t in range(nt):
                        ps_ = PS[6 + t % 2]
                        for q4 in range(4):
                            tk = t * 4 + q4
                            for c in range(2):
                                S.op(S.pe, lambda c=c, tk=tk, q4=q4, ps_=ps_: nc.tensor.matmul(
                                    ps_[:, q4 * 128:(q4 + 1) * 128], lhsT=ckvn[:, c, tk * 128:(tk + 1) * 128],
                                    rhs=wkv[:, c, h * 256 + 128:h * 256 + 256], start=(c == 0), stop=(c == 1)),
                                    reads=[wkv.b, ckvn.b], writes=[ps_.b], sig=(c == 1 and q4 == 3))
                        S.op(S.dve, lambda t=t, ps_=ps_: nc.vector.tensor_copy(
                            out=vh[:, t * 4:(t + 1) * 4, :], in_=ps_[:, :].rearrange("p (a b) -> p a b", a=4)),
                            reads=[ps_.b], writes=[vh.b])
                qsl = slice(qt * TT, (qt + 1) * TT)
                qnt = qn[u % 2]
                qrt = qr[u % 2]
                for c in range(4):
                    S.op(S.pe, lambda c=c: nc.tensor.matmul(
                        PS[6][:, :], lhsT=wq[:, c, h * 192:h * 192 + 128], rhs=cqn[:, c, qsl],
                        start=(c == 0), stop=(c == 3)), reads=[wq.b, cqn.b], writes=[PS[6].b], sig=(c == 3))
                S.op(S.act, lambda: nc.scalar.copy(out=qnt[:, :], in_=PS[6][:, :]), reads=[PS[6].b], writes=[qnt.b])
                for c in range(4):
                    S.op(S.pe, lambda c=c: nc.tensor.matmul(
                        PS[7][:, :], lhsT=wq[:, c, h * 192 + 128:h * 192 + 256], rhs=cqn[:, c, qsl],
                        start=(c == 0), stop=(c == 3)), reads=[wq.b, cqn.b], writes=[PS[7].b], sig=(c == 3))
                S.op(S.dve, lambda: nc.vector.tensor_tensor(out=u1[:, :], in0=PS[7][0:64, :], in1=cos[:, qsl], op=ALU.mult),
                     reads=[PS[7].b, cos.b], writes=[u1.b])
                for c in range(4):
                    S.op(S.pe, lambda c=c: nc.tensor.matmul(
                        PS[7][:, :], lhsT=wq[:, c, 1536 + h * 64:1536 + h * 64 + 128], rhs=cqn[:, c, qsl],
                        start=(c == 0), stop=(c == 3)), reads=[wq.b, cqn.b], writes=[PS[7].b], sig=(c == 3))
                S.op(S.dve, lambda: nc.vector.tensor_tensor(out=u2[:, :], in0=PS[7][0:64, :], in1=sin[:, qsl], op=ALU.mult),
                     reads=[PS[7].b, sin.b], writes=[u2.b])
                S.op(S.pool, lambda: nc.gpsimd.tensor_tensor(out=qrt[0:64, :], in0=u1[:, :], in1=u2[:, :], op=ALU.add),
                     reads=[u1.b, u2.b], writes=[qrt.b])

            ones32 = C.sb(pb, "ones32", [128, 128], F32)
            S.op(S.pool, lambda: nc.gpsimd.memset(ones32[:, :], 1.0), writes=[ones32.b])
            NA, NB = 3, 2
            accA = [[C.sb(pb, "accA", [128, TT], F32) for _ in range(NA)] for _ in range(2)]
            prep(0)
            pi = [0]
            sidx = [0]
            sbank = {}
            pend = []

            def qk(u, kc):
                h, qt = units[u]
                kTh = kT[h % 2]
                s_ps = PS[sidx[0] % 3]
                sbank[(u, kc)] = s_ps
                sidx[0] += 1
                ksl = slice(kc * 128, (kc + 1) * 128)
                S.op(S.pe, lambda: nc.tensor.matmul(s_ps[:, :], lhsT=kTh[:, ksl], rhs=qn[u % 2][:, :], start=True, stop=False),
                     reads=[kTh.b, qn[u % 2].b], writes=[s_ps.b], sig=False)
                S.op(S.pe, lambda: nc.tensor.matmul(s_ps[:, :], lhsT=krot[:, ksl], rhs=qr[u % 2][:, :], start=False, stop=True),
                     reads=[krot.b, qr[u % 2].b], writes=[s_ps.b], sig=True)

            def finish(u):
                h, qt = units[u]
                o_ps = PS[3 + u % 2]
                sum_ps = PS[5]
                for a_ in accA[u % 2][1:min(NA, used[u][0])]:
                    S.op(S.dve, lambda a_=a_: nc.vector.tensor_tensor(out=accA[u % 2][0][:, :], in0=accA[u % 2][0][:, :], in1=a_[:, :],
                                                                      op=ALU.add), reads=[a_.b, accA[u % 2][0].b], writes=[accA[u % 2][0].b])
                aA = accA[u % 2][0]
                S.op(S.pe, lambda: nc.tensor.matmul(sum_ps[:, :], lhsT=ones32[:, :], rhs=aA[:, :], start=(used[u][1] == 0), stop=True),
                     reads=[ones32.b, aA.b], writes=[sum_ps.b], sig=True)
                ont = on[u % 2]
                S.op(S.dve, lambda: nc.vector.reciprocal(out=rinv[:, :], in_=sum_ps[:, :]),
                     reads=[sum_ps.b], writes=[rinv.b])
                S.op(S.dve, lambda: nc.vector.tensor_tensor(out=ont[:, :], in0=o_ps[:, :], in1=rinv[:, :], op=ALU.mult),
                     reads=[o_ps.b, rinv.b], writes=[ont.b])
                S.dma(S.sp, oT_d[qt * 4:(qt + 1) * 4, :, h, :].rearrange("c p t -> p c t"),
                      ont[:, :].rearrange("p (c t) -> p c t", c=4), reads=[ont.b], writes=[oT_d.b], slot=ont.b)

            seq = [(u, kc) for u in range(len(units)) for kc in range(nkc)]
            LOOK = 2
            cntA, cntB = [0], [0]
            used = {}
            for i in range(min(LOOK, len(seq))):
                qk(*seq[i])
            for i, (u, kc) in enumerate(seq):
                h, qt = units[u]
                vh = vv[h % 2]
                o_ps = PS[3 + u % 2]
                if kc == 0:
                    cntA[0] = 0
                    cntB[0] = 0
                if kc == min(4, nkc - 1 - LOOK) and u + 1 < len(units):
                    prep(u + 1)
                if kc == min(2, nkc - 1) and pend:
                    finish(pend.pop())
                if i + LOOK < len(seq):
                    qk(*seq[i + LOOK])
                s_ps = sbank.pop((u, kc))
                pTt = pT[pi[0] % 6]
                pi[0] += 1
                S.op(S.act, lambda s_ps=s_ps, pTt=pTt: nc.scalar.activation(
                    out=pTt[:, :], in_=s_ps[:, :], func=AF.Exp, scale=SC),
                    reads=[s_ps.b], writes=[pTt.b])
                S.op(S.pe, lambda pTt=pTt, kc=kc, vh=vh, o_ps=o_ps: nc.tensor.matmul(
                    o_ps[:, :], lhsT=vh[:, kc, :], rhs=pTt[:, :], start=(kc == 0), stop=(kc == nkc - 1)),
                    reads=[vh.b, pTt.b], writes=[o_ps.b], sig=True)
                sum_ps = PS[5]
                if kc % 8 == 7:
                    S.op(S.pe, lambda pTt=pTt, first=(cntB[0] == 0): nc.tensor.matmul(
                        sum_ps[:, :], lhsT=ones[:, :], rhs=pTt[:, :], start=first, stop=False),
                        reads=[ones.b, pTt.b], writes=[sum_ps.b], sig=True)
                    cntB[0] += 1
                else:
                    acc = accA[u % 2][cntA[0] % NA]
                    first = cntA[0] < NA
                    cntA[0] += 1
                    if first:
                        S.op(S.dve, lambda acc=acc, pTt=pTt: nc.vector.tensor_copy(out=acc[:, :], in_=pTt[:, :]),
                             reads=[pTt.b], writes=[acc.b])
                    else:
                        S.op(S.dve, lambda acc=acc, pTt=pTt: nc.vector.tensor_tensor(out=acc[:, :], in0=acc[:, :], in1=pTt[:, :], op=ALU.add),
                             reads=[pTt.b, acc.b], writes=[acc.b])
                if kc == nkc - 1:
                    used[u] = (cntA[0], cntB[0])
                    pend.append(u)
            while pend:
                finish(pend.pop())
            C.end(pb)
        C.end(st)
    if hook is not None:
        hook("alloc")
    with ExitStack() as pc:
        wo = C.sb(pc, "wo", [128, 8, D], BF16)
        load_w1(C, wo, wo[:, :, :], W["mla_w_o"][j])
        gbc, bbc = load_ln_consts(C, pc, W, li, 0)
        if hook is not None:
            hook("load")
        oTs = [C.sb(pc, "oTs", [128, 8, 128], BF16) for _ in range(3)]
        xrs = [C.sb(pc, "xr", [128, D], F32) for _ in range(2)]
        z = [C.sb(pc, "z", [128, D], F32) for _ in range(2)]
        stt = [C.sb(pc, "stt", [128, 16], F32) for _ in range(2)]
        for tk in range(nkc):
            ot = oTs[tk % 3]
            S.dma(S.sp, ot[:, :, :], oT_d[tk, :, :, :], reads=[oT_d.b], writes=[ot.b], slot=ot.b)
            o_ps = [PS[(tk % 2) * 2], PS[(tk % 2) * 2 + 1]]
            for hf in range(2):
                for h in range(8):
                    S.op(S.pe, lambda hf=hf, h=h, ot=ot, o_ps=o_ps: nc.tensor.matmul(
                        o_ps[hf][:, :], lhsT=ot[:, h, :], rhs=wo[:, h, hf * 512:(hf + 1) * 512],
                        start=(h == 0), stop=(h == 7)), reads=[ot.b, wo.b], writes=[o_ps[hf].b], sig=(h == 7))
            ln_epilogue(C, o_ps, xsrc, xrs[tk % 2], gbc, bbc, z[tk % 2], stt[tk % 2], xdst, tk * 128,
                        pool_free=(hook is not None and tk < 16))
        C.end(pc)


def lin_sublayer(C, W, li, xsrc, xdst, SEQ, PS, ident, CONST, SCR, kind, hook=None):
    S = C.S
    nc = C.nc
    ret = (kind == "ret")
    H = 4
    DKC = 2 if ret else 1
    DV = 512 if ret else 256
    NQ = H * DKC
    QT = NQ * 128
    VT = H * DV
    w_in = W["ret_w_in"] if ret else W["gla_w_in"]
    w_o = W["ret_w_o"] if ret else W["gla_w_o"]
    KOFF, VOFF, GOFF = QT, 2 * QT, 2 * QT + VT
    N = SEQ // 128
    y_d, R_d, cs_d, V_d = SCR["y"], SCR["R"], SCR["cs"], SCR["V"]
    with ExitStack() as st:
        mask = C.sb(st, "mask", [128, 256], F32)
        S.dma(S.sp, mask[:, :], CONST["mask"][:, :], reads=[], writes=[mask.b], slot=mask.b)
        def ret_rope():
            with ExitStack() as tst:
                cos, sin = rope_tables(C, tst, tst, CONST["pos"], CONST["invf128"], 128, SEQ)
                S.dma(S.sp, cs_d[0, :, :], cos[:, :], reads=[cos.b], writes=[cs_d.b], slot=cos.b)
                S.dma(S.sp, cs_d[1, :, :], sin[:, :], reads=[sin.b], writes=[cs_d.b], slot=sin.b)
                C.end(tst)

        if ret:
            io = C.sb(st, "io", [128, 4, 128], F32)
            S.dma(S.sp, io[:, :, :], CONST["io"][:, :, :], reads=[], writes=[io.b], slot=io.b)
            lg = C.sb(st, "lg", [128, 8], F32)
            lpos = C.sb(st, "lpos", [128, 8], F32)
            lneg = C.sb(st, "lneg", [128, 8], F32)
            S.dma(S.sp, lg[:, :], W["ret_decay_logit"][0].rearrange("a b -> (a b)").partition_broadcast(128),
                  reads=[], writes=[lg.b], slot=lg.b)
            S.op(S.act, lambda: nc.scalar.activation(out=lg[:, :], in_=lg[:, :], func=AF.Exp, scale=-1.0),
                 reads=[lg.b], writes=[lg.b])
            S.op(S.act, lambda: nc.scalar.activation(out=lpos[:, :], in_=lg[:, :], func=AF.Ln, bias=1.0, scale=1.0),
                 reads=[lg.b], writes=[lpos.b])
            S.op(S.act, lambda: nc.scalar.mul(out=lneg[:, :], in_=lpos[:, :], mul=-1.0), reads=[lpos.b], writes=[lneg.b])
            tabs = C.sb(st, "tabs", [128, 6, H, 128], F32)
            edge = C.sb(st, "edge", [128, 8], F32)
            kb = C.sb(st, "kb", [128, 1], F32)
            S.op(S.pool, lambda: nc.gpsimd.memset(kb[:, :], math.log(256.0 ** -0.5)), writes=[kb.b])
            zb = C.sb(st, "zb", [128, 1], F32)
            S.op(S.pool, lambda: nc.gpsimd.memset(zb[:, :], 0.0), writes=[zb.b])
            for z in range(2):
                for vi, (ioi, sgn, bias) in enumerate(((0 if z == 0 else 2, lneg, zb), (0 if z == 0 else 2, lpos, kb),
                                                       (1 if z == 0 else 3, lneg, kb))):
                    for h in range(H):
                        col = z * 4 + h
                        S.op(S.act, lambda z=z, vi=vi, h=h, ioi=ioi, sgn=sgn, col=col, bias=bias: nc.scalar.activation(
                            out=tabs[:, z * 3 + vi, h, :], in_=io[:, ioi, :], func=AF.Exp, scale=sgn[:, col:col + 1],
                            bias=bias[:, 0:1]), reads=[io.b, sgn.b, bias.b], writes=[tabs.b])
            S.op(S.act, lambda: nc.scalar.activation(out=edge[:, :], in_=lneg[:, :], func=AF.Exp, scale=128.0),
                 reads=[lneg.b], writes=[edge.b])
            dij = C.sb(st, "dij", [128, 128], F32)
            pcol = C.sb(st, "pcol", [128, 2], F32)
            S.dma(S.sp, dij[:, :], CONST["dij"][:, :], reads=[], writes=[dij.b], slot=dij.b)
            S.dma(S.sp, pcol[:, :], CONST["pcol"][:, :], reads=[], writes=[pcol.b], slot=pcol.b)
            Dp = C.sb(st, "Dp", [128, H, 128], F32)
            dtmp = C.sb(st, "dtmp", [128, 2, 128], F32)
            decK = C.sb(st, "decK", [128, 8], F32)
            for h in range(H):
                S.op(S.act, lambda h=h: nc.scalar.activation(out=decK[:, h:h + 1], in_=lpos[:, h:h + 1], func=AF.Exp,
                                                             scale=pcol[:, 1:2], bias=lpos[:, h:h + 1]),
                     reads=[lpos.b, pcol.b], writes=[decK.b])
                S.op(S.dve, lambda h=h: nc.vector.tensor_scalar(out=dtmp[:, 0, :], in0=mask[:, 0:128], scalar1=decK[:, h:h + 1],
                                                                scalar2=256.0 ** -0.5, op0=ALU.mult, op1=ALU.mult),
                     reads=[mask.b, decK.b], writes=[dtmp.b])
                S.op(S.act, lambda h=h: nc.scalar.activation(out=dtmp[:, 1, :], in_=dij[:, :], func=AF.Exp,
                                                             scale=lpos[:, 4 + h:5 + h]), reads=[dij.b, lpos.b], writes=[dtmp.b])
                S.op(S.dve, lambda: nc.vector.tensor_tensor(out=dtmp[:, 1, :], in0=dtmp[:, 1, :], in1=mask[:, 128:256], op=ALU.mult),
                     reads=[dtmp.b, mask.b], writes=[dtmp.b])
                S.op(S.dve, lambda h=h: nc.vector.tensor_tensor(out=dtmp[:, 1, :], in0=dtmp[:, 1, :], in1=tabs[:, 1, h, :], op=ALU.mult),
                     reads=[dtmp.b, tabs.b], writes=[dtmp.b])
                S.op(S.dve, lambda h=h: nc.vector.tensor_tensor(out=Dp[:, h, :], in0=dtmp[:, 0, :], in1=dtmp[:, 1, :], op=ALU.add),
                     reads=[dtmp.b], writes=[Dp.b])
            for z in range(2):
                for h in range(H):
                    col = z * 4 + h
                    S.op(S.act, lambda z=z, col=col: nc.scalar.activation(out=decK[:, col:col + 1], in_=lneg[:, col:col + 1], func=AF.Exp,
                                                                          scale=pcol[:, z:z + 1], bias=kb[:, 0:1]),
                         reads=[lneg.b, pcol.b, kb.b, decK.b], writes=[decK.b])
        else:
            ones32 = C.sb(st, "ones32", [128, 128], F32)
            S.op(S.pool, lambda: nc.gpsimd.memset(ones32[:, :], 1.0), writes=[ones32.b])
            wa1 = C.sb(st, "wa1", [128, 2, 8, 16], BF16)
            wa2 = C.sb(st, "wa2", [16, 2, 512], BF16)
            ba = C.sb(st, "ba", [128, 8], F32)
            for z in range(2):
                S.dma(S.pool, wa1[:, z, :, :], W["gla_w_a1"][0, z].rearrange("(kc p) r -> p kc r", p=128),
                      reads=[], writes=[wa1.b], slot=wa1.b)
                S.dma(S.pool, wa2[:, z, :], W["gla_w_a2"][0, z], reads=[], writes=[wa2.b], slot=wa2.b)
                for h in range(H):
                    S.dma(S.sp, ba[:, z * 4 + h:z * 4 + h + 1],
                          W["gla_b_a"][0, z, h * 128:(h + 1) * 128].rearrange("(p o) -> p o", o=1),
                          reads=[], writes=[ba.b], slot=ba.b)

        def phase12(ph, mode):
            if mode == 1:
                wA = C.sb(ph, "wkv", [128, 8, QT + VT], BF16)
                load_w1(C, wA, wA[:, :, :], w_in[0, :, KOFF:GOFF])
                if ret:
                    ret_rope()
                KC0, QC0 = 0, None
            else:
                wA = C.sb(ph, "wqk", [128, 8, 2 * QT], BF16)
                load_w1(C, wA, wA[:, :, :], w_in[0, :, 0:2 * QT])
                KC0, QC0 = QT, 0
            xts = [None, None]
            NXB = 6
            xbf = [C.sb(ph, "xbf", [128, D], BF16) for _ in range(NXB)]
            xT4 = C.sb(ph, "xT4", [128, 8, 512], BF16)
            rawK = C.sb(ph, "rawK", [128, NQ, 512], F32)
            rawQ = C.sb(ph, "rawQ", [128, NQ, 512], F32) if mode == 2 else None
            tmp = C.sb(ph, "tmp", [128, NQ, 128], F32)
            rotk = C.sb(ph, "rotk", [128, NQ, 128], F32) if ret else None
            rotq = C.sb(ph, "rotq", [128, NQ, 128], F32)
            if ret:
                names = ("Kp",) if mode == 1 else ("Qf", "Qb", "Kp")
            else:
                names = ("KSb",) if mode == 1 else ("Qf", "Qb", "Kf", "Kb", "KSf")
            var = {nm: [C.sb(ph, nm, [128, NQ, 128], BF16) for _ in range(2)] for nm in names}
            kstm1 = C.sb(ph, "kstm", [128, QT], BF16)
            kstm = [kstm1, kstm1]
            vtm = [C.sb(ph, "vtm", [128, VT], BF16) for _ in range(2)]
            cst = [C.sb(ph, "cst", [128, 2, H, 128], F32) for _ in range(2)] if ret else None
            st32 = C.sb(ph, "st32", [128, NQ, DV], F32)
            stbf = [C.sb(ph, "stbf", [128, NQ, DV], BF16) for _ in range(2)]
            rbf = [C.sb(ph, "rbf", [128, NQ, DV], BF16) for _ in range(2)] if mode == 2 else None
            msk = [C.sb(ph, "msk", [128, 256], BF16) for _ in range(2)]
            yst = [C.sb(ph, "yst", [128, VT], F32) for _ in range(1)] if mode == 2 else None
            S.op(S.pool, lambda: nc.gpsimd.memset(st32[:, :, :], 0.0), writes=[st32.b])
            S.op(S.pool, lambda: nc.gpsimd.memset(stbf[0][:, :, :], 0.0), writes=[stbf[0].b])
            if not ret:
                gl = C.sb(ph, "gl", [128, H, 128], F32)
                gL = C.sb(ph, "gL", [128, H, 128], F32)
                gE = {k: C.sb(ph, "gE" + k, [128, H, 128], F32) for k in ("a", "b", "c")}
                gtl = C.sb(ph, "gtl", [16, 128], BF16)
                gedge = [C.sb(ph, "gedge", [128, 8], F32) for _ in range(2)]
                ginv = C.sb(ph, "ginv", [128, 4], F32)
            order = list(range(N)) if mode == 2 else list(range(N - 1, -1, -1))
            for i in range(min(NXB, N)):
                x_fetch(C, xsrc, order[i] * 128, xts[i % 2], xbf[i % NXB])
            alt = [0]

            def ew(fn, reads, writes, force=None):
                if force is None:
                    eng = S.dve if alt[0] % 2 == 0 else S.pool
                    alt[0] += 1
                else:
                    eng = force
                h = nc.vector if eng is S.dve else nc.gpsimd
                S.op(eng, lambda: fn(h), reads=reads, writes=writes)

            def project(coff, dst):
                for oc in range(NQ):
                    ps_ = PS[oc % 2]
                    for kc in range(8):
                        S.op(S.pe, lambda oc=oc, kc=kc, ps_=ps_: nc.tensor.matmul(
                            ps_[:, :], lhsT=wA[:, kc, coff + oc * 128:coff + (oc + 1) * 128], rhs=xT4[:, kc, :],
                            start=(kc == 0), stop=(kc == 7)), reads=[wA.b, xT4.b], writes=[ps_.b], sig=(kc == 7))
                    S.op(S.act, lambda oc=oc, ps_=ps_: nc.scalar.copy(out=dst[:, oc, :], in_=ps_[:, :]),
                         reads=[ps_.b], writes=[dst.b])

            def rotary(src, csl, dst, cs, out=None):
                raw = src
                sv = src[:, :, csl].rearrange("p (h c) t -> p h c t", c=2)
                dv_ = dst[:, :, :].rearrange("p (h c) t -> p h c t", c=2)
                tv = tmp[:, :, :].rearrange("p (h c) t -> p h c t", c=2)
                cosb = cs[:, 0, :, :]
                sinb = cs[:, 1, :, :]
                x1, x2 = sv[:, :, 0, :], sv[:, :, 1, :]
                ew(lambda e: e.tensor_tensor(out=dv_[:, :, 0, :], in0=x1, in1=cosb, op=ALU.mult), [raw.b, cs.b], [dst.b])
                ew(lambda e: e.tensor_tensor(out=tv[:, :, 0, :], in0=x2, in1=sinb, op=ALU.mult), [raw.b, cs.b], [tmp.b])
                ew(lambda e: e.tensor_tensor(out=dv_[:, :, 1, :], in0=x1, in1=sinb, op=ALU.mult), [raw.b, cs.b], [dst.b])
                ew(lambda e: e.tensor_tensor(out=tv[:, :, 1, :], in0=x2, in1=cosb, op=ALU.mult), [raw.b, cs.b], [tmp.b])
                o_ = dst if out is None else out
                ov = o_[:, :, :].rearrange("p (h c) t -> p h c t", c=2)
                ew(lambda e: e.tensor_tensor(out=ov[:, :, 0, :], in0=dv_[:, :, 0, :], in1=tv[:, :, 0, :], op=ALU.subtract),
                   [dst.b, tmp.b], [o_.b])
                ew(lambda e: e.tensor_tensor(out=ov[:, :, 1, :], in0=dv_[:, :, 1, :], in1=tv[:, :, 1, :], op=ALU.add),
                   [dst.b, tmp.b], [o_.b])

            def variant(src_ap, src_b, dst, tab_ap, tab_b):
                if DKC == 1:
                    ew(lambda e: e.tensor_tensor(out=dst[:, :, :], in0=src_ap, in1=tab_ap, op=ALU.mult),
                       [src_b, tab_b], [dst.b])
                else:
                    sv = src_ap.rearrange("p (h c) t -> p h c t", c=DKC)
                    dv_ = dst[:, :, :].rearrange("p (h c) t -> p h c t", c=DKC)
                    for c in range(DKC):
                        ew(lambda e, c=c: e.tensor_tensor(out=dv_[:, :, c, :], in0=sv[:, :, c, :], in1=tab_ap, op=ALU.mult),
                           [src_b, tab_b], [dst.b])

            def gla_gates(z, xTt, ged):
                for kc in range(8):
                    S.op(S.pe, lambda kc=kc: nc.tensor.matmul(PS[1][0:16, 0:128], lhsT=wa1[:, z, kc, :], rhs=xT4[:, kc, xTt],
                                                              start=(kc == 0), stop=(kc == 7)),
                         reads=[wa1.b, xT4.b], writes=[PS[1].b], sig=(kc == 7))
                S.op(S.act, lambda: nc.scalar.copy(out=gtl[:, :], in_=PS[1][0:16, 0:128]), reads=[PS[1].b], writes=[gtl.b])
                for h in range(H):
                    S.op(S.pe, lambda h=h: nc.tensor.matmul(PS[7][:, h * 128:(h + 1) * 128], lhsT=wa2[:, z, h * 128:(h + 1) * 128],
                                                            rhs=gtl[:, :], start=True, stop=True),
                         reads=[wa2.b, gtl.b], writes=[PS[7].b], sig=(h == H - 1))
                bab = ba[:, z * 4:z * 4 + 4].unsqueeze(2).to_broadcast([128, H, 128])
                S.op(S.dve, lambda: nc.vector.tensor_tensor(out=gl[:, :, :], in0=PS[7][:, :].rearrange("p (h t) -> p h t", h=H),
                                                            in1=bab, op=ALU.add), reads=[PS[7].b, ba.b], writes=[gl.b])
                S.op(S.act, lambda: nc.scalar.activation(out=gl[:, :, :], in_=gl[:, :, :], func=AF.Exp, scale=-1.0),
                     reads=[gl.b], writes=[gl.b])
                S.op(S.act, lambda: nc.scalar.activation(out=gl[:, :, :], in_=gl[:, :, :], func=AF.Ln, bias=1.0, scale=1.0),
                     reads=[gl.b], writes=[gl.b])
                for h in range(H):
                    S.op(S.dve, lambda h=h: nc.vector.tensor_tensor_scan(out=gL[:, h, :], data0=ones32[:, :], data1=gl[:, h, :],
                                                                         initial=0.0, op0=ALU.mult, op1=ALU.add),
                         reads=[gl.b, ones32.b], writes=[gL.b])
                c0 = z * 4
                S.op(S.act, lambda: nc.scalar.activation(out=ged[:, c0:c0 + 4], in_=gL[:, :, 127], func=AF.Exp, scale=-1.0 / 16),
                     reads=[gL.b], writes=[ged.b])
                gb = ged[:, c0:c0 + 4].unsqueeze(2).to_broadcast([128, H, 128])
                if z == 0:
                    S.op(S.act, lambda: nc.scalar.activation(out=gE["a"][:, :, :], in_=gL[:, :, :], func=AF.Exp, scale=1.0 / 16),
                         reads=[gL.b], writes=[gE["a"].b])
                    S.op(S.act, lambda: nc.scalar.activation(out=gE["b"][:, :, :], in_=gL[:, :, :], func=AF.Exp, scale=-1.0 / 16),
                         reads=[gL.b], writes=[gE["b"].b])
                    ew(lambda e: e.tensor_tensor(out=gE["c"][:, :, :], in0=gE["a"][:, :, :], in1=gb, op=ALU.mult),
                       [gE["a"].b, ged.b], [gE["c"].b])
                    return {"EN": gE["a"], "EB": gE["b"], "ES": gE["c"]}
                S.op(S.act, lambda: nc.scalar.activation(out=ginv[:, :], in_=gL[:, :, 127], func=AF.Exp, scale=1.0 / 16),
                     reads=[gL.b], writes=[ginv.b])
                S.op(S.dve, lambda: nc.vector.tensor_tensor(out=gL[:, :, :], in0=gl[:, :, :], in1=gL[:, :, :], op=ALU.subtract),
                     reads=[gl.b, gL.b], writes=[gL.b])
                S.op(S.act, lambda: nc.scalar.activation(out=gE["c"][:, :, :], in_=gL[:, :, :], func=AF.Exp, scale=1.0 / 16),
                     reads=[gL.b], writes=[gE["c"].b])
                res = {"ES": gE["c"]}
                if mode == 2:
                    S.op(S.act, lambda: nc.scalar.activation(out=gE["b"][:, :, :], in_=gL[:, :, :], func=AF.Exp, scale=-1.0 / 16),
                         reads=[gL.b], writes=[gE["b"].b])
                    ib = ginv[:, :].unsqueeze(2).to_broadcast([128, H, 128])
                    ew(lambda e: e.tensor_tensor(out=gE["a"][:, :, :], in0=gE["c"][:, :, :], in1=ib, op=ALU.mult),
                       [gE["c"].b, ginv.b], [gE["a"].b])
                    ew(lambda e: e.tensor_tensor(out=gE["b"][:, :, :], in0=gE["b"][:, :, :], in1=gb, op=ALU.mult),
                       [gE["b"].b, ged.b], [gE["b"].b])
                    res["EN"] = gE["a"]
                    res["EB"] = gE["b"]
                return res

            def tab(z, nm, gt):
                if ret:
                    vi = z * 3 + {"EB": 0, "EN": 1, "ES": 2}[nm]
                    return tabs[:, vi, :, :], tabs.b
                return gt[nm][:, :, :], gt[nm].b

            def kst_transpose(srcv, dstk, dec=None):
                trv = PS[6].t[:, :].bitcast(BF16)
                for hc in range(NQ):
                    S.op(S.pe, lambda hc=hc: nc.tensor.transpose(trv[:, hc * 128:(hc + 1) * 128], srcv[:, hc, :], ident[:, :]),
                         reads=[srcv.b, ident.b], writes=[PS[6].b], sig=(hc == NQ - 1))
                if dec is None:
                    S.op(S.act, lambda: nc.scalar.copy(out=dstk[:, :], in_=trv[:, 0:QT]), reads=[PS[6].b], writes=[dstk.b])
                else:
                    for h in range(H):
                        S.op(S.act, lambda h=h: nc.scalar.activation(
                            out=dstk[:, h * DKC * 128:(h + 1) * DKC * 128], in_=trv[:, h * DKC * 128:(h + 1) * DKC * 128],
                            func=AF.Copy, scale=decK[:, dec * 4 + h:dec * 4 + h + 1]),
                            reads=[PS[6].b, decK.b], writes=[dstk.b])

            def prep(oi):
                n = order[oi]
                b2 = oi % 2
                csl = slice((n % 4) * 128, (n % 4 + 1) * 128)
                if oi % 4 == 0:
                    for j in range(oi, min(oi + 4, N)):
                        x_transpose(C, xbf[j % NXB], PS[6], xT4, (order[j] % 4) * 128, ident)
                    for j in range(oi + NXB, min(oi + NXB + 4, N)):
                        x_fetch(C, xsrc, order[j] * 128, xts[j % 2], xbf[j % NXB])
                    project(KC0, rawK)
                    if mode == 2:
                        project(QC0, rawQ)
                if ret:
                    for a_ in range(2):
                        S.dma(S.sp, cst[b2][:, a_, :, :],
                              cs_d[a_, :, n * 128:(n + 1) * 128].unsqueeze(1).to_broadcast([128, H, 128]),
                              reads=[cs_d.b], writes=[cst[b2].b], slot=cst[b2].b)
                if mode == 1:
                    for v4 in range(VT // 512):
                        ps_ = PS[v4 % 2]
                        for kc in range(8):
                            S.op(S.pe, lambda kc=kc, v4=v4, ps_=ps_: nc.tensor.matmul(
                                ps_[:, :], lhsT=xT4[:, kc, csl], rhs=wA[:, kc, QT + v4 * 512:QT + (v4 + 1) * 512],
                                start=(kc == 0), stop=(kc == 7)), reads=[wA.b, xT4.b], writes=[ps_.b], sig=(kc == 7))
                        S.op(S.act, lambda v4=v4, ps_=ps_: nc.scalar.copy(out=vtm[b2][:, v4 * 512:(v4 + 1) * 512], in_=ps_[:, :]),
                             reads=[ps_.b], writes=[vtm[b2].b])
                    S.dma(S.act, V_d[n, :, :], vtm[b2][:, :], reads=[vtm[b2].b], writes=[V_d.b], slot=vtm[b2].b)
                else:
                    S.dma(S.sp, vtm[b2][:, :], V_d[n, :, :], reads=[V_d.b], writes=[vtm[b2].b], slot=vtm[b2].b)
                    S.dma(S.sp, rbf[b2][:, :, :], R_d[n, :, :, :], reads=[R_d.b], writes=[rbf[b2].b], slot=rbf[b2].b)
                if ret:
                    rotary(rawK, csl, rotk, cst[b2], out=var["Kp"][b2])
                    if mode == 2:
                        rotary(rawQ, csl, rotq, cst[b2])
                        variant(rotq[:, :, :], rotq.b, var["Qf"][b2], *tab(0, "EB", None))
                        variant(rotq[:, :, :], rotq.b, var["Qb"][b2], *tab(1, "EB", None))
                    return
                ks_ap, ks_b = rawK[:, :, csl], rawK.b
                ged = None if ret else gedge[b2]
                if mode == 1:
                    gt = None if ret else gla_gates(1, csl, ged)
                    variant(ks_ap, ks_b, var["KSb"][b2], *tab(1, "ES", gt))
                    return
                gt = None if ret else gla_gates(0, csl, ged)
                variant(ks_ap, ks_b, var["Kf"][b2], *tab(0, "EN", gt))
                variant(ks_ap, ks_b, var["KSf"][b2], *tab(0, "ES", gt))
                if ret:
                    variant(ks_ap, ks_b, var["Kb"][b2], *tab(1, "EN", None))
                    rotary(rawQ, csl, rotq, cst[b2])
                    variant(rotq[:, :, :], rotq.b, var["Qf"][b2], *tab(0, "EB", None))
                    variant(rotq[:, :, :], rotq.b, var["Qb"][b2], *tab(1, "EB", None))
                else:
                    S.op(S.act, lambda: nc.scalar.mul(out=rotq[:, :, :], in_=rawQ[:, :, csl], mul=128.0 ** -0.5),
                         reads=[rawQ.b], writes=[rotq.b])
                    variant(rotq[:, :, :], rotq.b, var["Qf"][b2], *tab(0, "EB", gt))
                    gt1 = gla_gates(1, csl, ged)
                    variant(ks_ap, ks_b, var["Kb"][b2], *tab(1, "EN", gt1))
                    variant(rotq[:, :, :], rotq.b, var["Qb"][b2], *tab(1, "EB", gt1))

            def state_update(z, oi, heads=None):
                b2 = oi % 2
                banks = [PS[7], PS[0], PS[1]] if mode == 2 else [PS[7], PS[4], PS[5]]
                for h in (range(H) if heads is None else heads):
                    for c in range(DKC):
                        hc = h * DKC + c
                        ps_ = banks[hc % 3]
                        S.op(S.pe, lambda hc=hc, h=h, ps_=ps_: nc.tensor.matmul(
                            ps_[:, 0:DV], lhsT=kstm[b2][:, hc * 128:(hc + 1) * 128], rhs=vtm[b2][:, h * DV:(h + 1) * DV],
                            start=True, stop=True), reads=[kstm[b2].b, vtm[b2].b], writes=[ps_.b])
                        if ret:
                            ea, eb = edge[:, z * 4 + h:z * 4 + h + 1], edge.b
                        else:
                            ea, eb = gedge[b2][:, z * 4 + h:z * 4 + h + 1], gedge[b2].b
                        S.op(S.dve, lambda hc=hc, ps_=ps_, ea=ea: nc.vector.scalar_tensor_tensor(
                            out=st32[:, hc, :], in0=st32[:, hc, :], scalar=ea, in1=ps_[:, 0:DV], op0=ALU.mult, op1=ALU.add),
                            reads=[st32.b, ps_.b, eb], writes=[st32.b])
                        sdst = stbf[(oi + 1) % 2]
                        S.op(S.act, lambda hc=hc, sdst=sdst: nc.scalar.copy(out=sdst[:, hc, :], in_=st32[:, hc, :]),
                             reads=[st32.b], writes=[sdst.b])

            def engine(oi):
                n = order[oi]
                b2 = oi % 2
                if mode == 1:
                    if ret:
                        kst_transpose(var["Kp"][b2], kstm[b2], dec=1)
                    else:
                        kst_transpose(var["KSb"][b2], kstm[b2])
                    S.dma(S.act, R_d[n, :, :, :], stbf[b2][:, :, :], reads=[stbf[b2].b], writes=[R_d.b], slot=stbf[b2].b)
                    state_update(1, oi)
                    return
                if ret:
                    kst_transpose(var["Kp"][b2], kstm[b2], dec=0)
                else:
                    kst_transpose(var["KSf"][b2], kstm[b2])
                sold = stbf[b2]
                rb = rbf[b2]
                ys = yst[0]
                vt = vtm[b2]
                for h in range(H):
                    sc_ps = PS[2 + h % 2]
                    for (z, kn, qn_) in (((0, "Kp", "Qf"),) if ret else ((0, "Kf", "Qf"), (1, "Kb", "Qb"))):
                        for c in range(DKC):
                            hc = h * DKC + c
                            S.op(S.pe, lambda z=z, kn=kn, qn_=qn_, hc=hc, c=c, sc_ps=sc_ps: nc.tensor.matmul(
                                sc_ps[:, z * 128:(z + 1) * 128], lhsT=var[kn][b2][:, hc, :], rhs=var[qn_][b2][:, hc, :],
                                start=(c == 0), stop=(c == DKC - 1)), reads=[var[kn][b2].b, var[qn_][b2].b], writes=[sc_ps.b],
                                sig=((z == 1 or ret) and c == DKC - 1))
                    mk_ = msk[h % 2]
                    y_ps = PS[4 + h % 2]
                    mms = []
                    if ret:
                        S.op(S.dve, lambda mk_=mk_, sc_ps=sc_ps, h=h: nc.vector.tensor_tensor(
                            out=mk_[:, 0:128], in0=sc_ps[:, 0:128], in1=Dp[:, h, :], op=ALU.mult),
                            reads=[sc_ps.b, Dp.b], writes=[mk_.b])
                        mms.append((mk_[:, 0:128], vt[:, h * DV:(h + 1) * DV], [mk_.b, vt.b]))
                    else:
                        S.op(S.dve, lambda mk_=mk_, sc_ps=sc_ps: nc.vector.tensor_tensor(out=mk_[:, :], in0=sc_ps[:, 0:256], in1=mask[:, :],
                                                                                         op=ALU.mult),
                             reads=[sc_ps.b, mask.b], writes=[mk_.b])
                        mms.append((mk_[:, 0:128], vt[:, h * DV:(h + 1) * DV], [mk_.b, vt.b]))
                        mms.append((mk_[:, 128:256], vt[:, h * DV:(h + 1) * DV], [mk_.b, vt.b]))
                    for c in range(DKC):
                        hc = h * DKC + c
                        mms.append((var["Qb"][b2][:, hc, :], rb[:, hc, :], [var["Qb"][b2].b, rb.b]))
                    for c in range(DKC):
                        hc = h * DKC + c
                        mms.append((var["Qf"][b2][:, hc, :], sold[:, hc, :], [var["Qf"][b2].b, sold.b]))
                    for mi, (l_, r_, rd) in enumerate(mms):
                        S.op(S.pe, lambda l_=l_, r_=r_, mi=mi, y_ps=y_ps: nc.tensor.matmul(
                            y_ps[:, 0:DV], lhsT=l_, rhs=r_, start=(mi == 0), stop=(mi == len(mms) - 1)),
                            reads=rd, writes=[y_ps.b], sig=(mi == len(mms) - 1))
                    S.op(S.act, lambda h=h, y_ps=y_ps, ys=ys: nc.scalar.copy(out=ys[:, h * DV:(h + 1) * DV], in_=y_ps[:, 0:DV]),
                         reads=[y_ps.b], writes=[ys.b])
                    state_update(0, oi, [h])
                S.dma(S.act, y_d[n * 128:(n + 1) * 128, :], ys[:, :], reads=[ys.b], writes=[y_d.b], slot=ys.b)

            prep(0)
            for oi in range(N):
                if oi + 1 < N:
                    prep(oi + 1)
                engine(oi)
            C.end(ph)

        with ExitStack() as ph:
            phase12(ph, 1)
        with ExitStack() as ph:
            phase12(ph, 2)
        C.end(st)
    if hook is not None:
        hook("alloc")
    with ExitStack() as ph:
        wg = C.sb(ph, "wg", [128, 8, VT], BF16)
        load_w1(C, wg, wg[:, :, :], w_in[0, :, GOFF:GOFF + VT])
        NF = VT // 128
        wo = C.sb(ph, "wo", [128, NF, D], BF16)
        load_w1(C, wo, wo[:, :, :], w_o[0])
        gbc, bbc = load_ln_consts(C, ph, W, li, 0)
        if ret:
            nw = C.sb(ph, "nw", [128, VT], F32)
            nb = C.sb(ph, "nb", [128, VT], F32)
            S.dma(S.sp, nw[:, :], W["ret_gn_w"][0, :].partition_broadcast(128), reads=[], writes=[nw.b], slot=nw.b)
            S.dma(S.sp, nb[:, :], W["ret_gn_b"][0, :].partition_broadcast(128), reads=[], writes=[nb.b], slot=nb.b)
        else:
            nw = C.sb(ph, "nw", [128, DV], F32)
            S.dma(S.sp, nw[:, :], W["gla_norm"][0, :].partition_broadcast(128), reads=[], writes=[nw.b], slot=nw.b)
        xts = [None, None]
        xbf = [C.sb(ph, "xbf", [128, D], BF16) for _ in range(2)]
        xT = [C.sb(ph, "xT", [128, 8, 128], BF16) for _ in range(2)]
        yl = [C.sb(ph, "yl", [128, VT], F32) for _ in range(2)]
        gs = [C.sb(ph, "gs", [128, VT], F32) for _ in range(2)]
        gy = [C.sb(ph, "gy", [128, VT], BF16) for _ in range(2)]
        gyT = [C.sb(ph, "gyT", [128, NF, 128], BF16) for _ in range(2)]
        nst = [C.sb(ph, "nst", [128, 32], F32) for _ in range(2)]
        junk = C.sb(ph, "junk", [128, DV], F32)
        xrs = [C.sb(ph, "xr", [128, D], F32) for _ in range(2)]
        z = [C.sb(ph, "z", [128, D], F32) for _ in range(2)]
        stt = [C.sb(ph, "stt", [128, 16], F32) for _ in range(2)]
        for i in range(min(2, N)):
            x_fetch(C, xsrc, i * 128, xts[i], xbf[i])

        def prep3(n):
            b2 = n % 2
            xTt = xT[b2]
            x_transpose(C, xbf[b2], PS[6], xTt, 0, ident)
            if n + 2 < N:
                x_fetch(C, xsrc, (n + 2) * 128, xts[b2], xbf[b2])
            y = yl[b2]
            S.dma(S.sp, y[:, :], y_d[n * 128:(n + 1) * 128, :], reads=[y_d.b], writes=[y.b], slot=y.b)
            for g4 in range(VT // 512):
                ps_ = PS[g4 % 2]
                for kc in range(8):
                    S.op(S.pe, lambda kc=kc, g4=g4, ps_=ps_: nc.tensor.matmul(
                        ps_[:, :], lhsT=xTt[:, kc, :], rhs=wg[:, kc, g4 * 512:(g4 + 1) * 512],
                        start=(kc == 0), stop=(kc == 7)), reads=[wg.b, xTt.b], writes=[ps_.b], sig=(kc == 7))
                S.op(S.act, lambda g4=g4, ps_=ps_: nc.scalar.activation(out=gs[b2][:, g4 * 512:(g4 + 1) * 512], in_=ps_[:, :], func=AF.Silu),
                     reads=[ps_.b], writes=[gs[b2].b])
            ns = nst[b2]
            for h in range(H):
                ysl = y[:, h * DV:(h + 1) * DV]
                if ret:
                    S.op(S.dve, lambda h=h, ysl=ysl: nc.vector.bn_stats(out=ns[:, h * 6:(h + 1) * 6], in_=ysl),
                         reads=[y.b], writes=[ns.b])
                    S.op(S.dve, lambda h=h: nc.vector.bn_aggr(out=ns[:, 24 + h * 2:26 + h * 2], in_=ns[:, h * 6:(h + 1) * 6]),
                         reads=[ns.b], writes=[ns.b])
                else:
                    S.op(S.act, lambda h=h, ysl=ysl: nc.scalar.activation(out=junk[:, :], in_=ysl, func=AF.Square,
                                                                          accum_out=ns[:, 24 + h * 2 + 1:24 + h * 2 + 2]),
                         reads=[y.b], writes=[ns.b, junk.b])
            vv_ = ns[:, 24:32].rearrange("p (h t) -> p h t", t=2)
            if ret:
                S.op(S.act, lambda: nc.scalar.activation(out=vv_[:, :, 1], in_=vv_[:, :, 1], func=AF.Sqrt, bias=LN_EPS, scale=1.0),
                     reads=[ns.b], writes=[ns.b])
            else:
                S.op(S.act, lambda: nc.scalar.activation(out=vv_[:, :, 1], in_=vv_[:, :, 1], func=AF.Sqrt, bias=RMS_EPS, scale=1.0 / DV),
                     reads=[ns.b], writes=[ns.b])
            S.op(S.dve, lambda: nc.vector.reciprocal(out=vv_[:, :, 1], in_=vv_[:, :, 1]), reads=[ns.b], writes=[ns.b])
            yv = y[:, :].rearrange("p (h e) -> p h e", h=H)
            if ret:
                S.op(S.dve, lambda: nc.vector.scalar_tensor_tensor(out=vv_[:, :, 0], in0=vv_[:, :, 0], scalar=-1.0, in1=vv_[:, :, 1],
                                                                   op0=ALU.mult, op1=ALU.mult), reads=[ns.b], writes=[ns.b])
                for h in range(H):
                    ysl = y[:, h * DV:(h + 1) * DV]
                    S.op(S.act, lambda h=h, ysl=ysl: nc.scalar.activation(out=ysl, in_=ysl, func=AF.Identity,
                                                                          bias=ns[:, 24 + 2 * h:25 + 2 * h], scale=ns[:, 25 + 2 * h:26 + 2 * h]),
                         reads=[y.b, ns.b], writes=[y.b])
                S.op(S.pool, lambda: nc.gpsimd.tensor_tensor(out=y[:, 0:VT // 2], in0=y[:, 0:VT // 2], in1=nw[:, 0:VT // 2], op=ALU.mult),
                     reads=[y.b, nw.b], writes=[y.b])
                S.op(S.dve, lambda: nc.vector.tensor_tensor(out=y[:, VT // 2:VT], in0=y[:, VT // 2:VT], in1=nw[:, VT // 2:VT], op=ALU.mult),
                     reads=[y.b, nw.b], writes=[y.b])
                S.op(S.dve, lambda: nc.vector.tensor_tensor(out=y[:, :], in0=y[:, :], in1=nb[:, :], op=ALU.add),
                     reads=[y.b, nb.b], writes=[y.b])
            else:
                for h in range(H):
                    ysl = y[:, h * DV:(h + 1) * DV]
                    S.op(S.act, lambda h=h, ysl=ysl: nc.scalar.activation(out=ysl, in_=ysl, func=AF.Copy,
                                                                          scale=ns[:, 25 + 2 * h:26 + 2 * h]),
                         reads=[y.b, ns.b], writes=[y.b])
                nwb = nw[:, :].unsqueeze(1).to_broadcast([128, H, DV])
                S.op(S.pool, lambda: nc.gpsimd.tensor_tensor(out=yv, in0=yv, in1=nwb, op=ALU.mult), reads=[y.b, nw.b], writes=[y.b])
            S.op(S.dve, lambda: nc.vector.tensor_tensor(out=gy[b2][:, :], in0=y[:, :], in1=gs[b2][:, :], op=ALU.mult),
                 reads=[y.b, gs[b2].b], writes=[gy[b2].b])

        def engine3(n):
            b2 = n % 2
            for r8 in range(0, NF, 8):
                trv = PS[6].t[:, :].bitcast(BF16)
                for fc in range(r8, r8 + 8):
                    S.op(S.pe, lambda fc=fc, r8=r8: nc.tensor.transpose(trv[:, (fc - r8) * 128:(fc - r8 + 1) * 128],
                                                                        gy[b2][:, fc * 128:(fc + 1) * 128], ident[:, :]),
                         reads=[gy[b2].b, ident.b], writes=[PS[6].b], sig=(fc == r8 + 7))
                S.op(S.act, lambda r8=r8: nc.scalar.copy(out=gyT[b2][:, r8:r8 + 8, :], in_=trv.rearrange("p (k j) -> p k j", k=8)),
                     reads=[PS[6].b], writes=[gyT[b2].b])
            o_ps = [PS[2 + b2 * 2], PS[3 + b2 * 2]]
            for hf in range(2):
                for fc in range(NF):
                    S.op(S.pe, lambda hf=hf, fc=fc, o_ps=o_ps: nc.tensor.matmul(
                        o_ps[hf][:, :], lhsT=gyT[b2][:, fc, :], rhs=wo[:, fc, hf * 512:(hf + 1) * 512],
                        start=(fc == 0), stop=(fc == NF - 1)), reads=[gyT[b2].b, wo.b], writes=[o_ps[hf].b], sig=(fc == NF - 1))
            ln_epilogue(C, o_ps, xsrc, xrs[b2], gbc, bbc, z[b2], stt[b2], xdst, n * 128)

        prep3(0)
        if hook is not None:
            hook("load")
        for n in range(N):
            if n + 1 < N:
                prep3(n + 1)
            engine3(n)
        C.end(ph)


WSPECS = {
    "ln_g": ([DEPTH, 2, D], F32), "ln_b": ([DEPTH, 2, D], F32),
    "ffn_w_in": ([DEPTH, D, 2 * FH], F32), "ffn_w_out": ([DEPTH, FH, D], F32),
    "mla_w_down": ([2, D, 832], F32), "mla_q_norm": ([2, 512], F32), "mla_w_uq": ([2, 512, 1536], F32),
    "mla_kv_norm": ([2, 256], F32), "mla_w_ukv": ([2, 256, 2048], F32), "mla_w_o": ([2, 1024, D], F32),
    "ret_w_in": ([1, D, 6144], F32), "ret_decay_logit": ([1, 2, 4], F32), "ret_gn_w": ([1, 2048], F32),
    "ret_gn_b": ([1, 2048], F32), "ret_w_o": ([1, 2048, D], F32),
    "gla_w_in": ([1, D, 3072], F32), "gla_w_a1": ([1, 2, D, 16], F32), "gla_w_a2": ([1, 2, 16, 512], F32),
    "gla_b_a": ([1, 2, 512], F32), "gla_norm": ([1, 256], F32), "gla_w_o": ([1, 1024, D], F32),
}


def const_inputs():
    ident = np.eye(128, dtype=np.float32).astype(ml_dtypes.bfloat16)
    f64 = (10000.0 ** (-np.arange(0, 64, 2, dtype=np.float32) / np.float32(64))).astype(np.float32)
    f256 = (10000.0 ** (-np.arange(0, 256, 2, dtype=np.float32) / np.float32(256))).astype(np.float32)
    ii = np.arange(128, dtype=np.float32)
    io = np.stack([ii + 1, 127 - ii, 128 - ii, ii], 0)[None].repeat(128, 0).astype(np.float32)
    jj = np.arange(128)[:, None]
    mf = (ii[None, :] >= jj).astype(np.float32)
    mb = (jj > ii[None, :]).astype(np.float32)
    mask = np.concatenate([mf, mb], 1).astype(np.float32)
    dij = (ii[None, :] - jj).astype(np.float32)
    pcol = np.stack([127 - ii, ii], 1).astype(np.float32)
    return {"c_dij": dij, "c_pcol": pcol, "c_io": io, "c_mask": mask, "c_ident": ident, "c_invf64": np.concatenate([f64, f64])[:, None].astype(np.float32),
            "c_invf128": f256[:, None].astype(np.float32)}


def build_program(SEQ, plan):
    nc = bass.Bass("TRN2", target_bir_lowering=False)
    with ExitStack() as stack:
        C = Ctx(nc, stack)
        S = C.S
        x_in = C.dram("x", [SEQ, D], F32, kind="ExternalInput")
        pos_in = C.dram("positions", [SEQ], I32, kind="ExternalInput")
        W = {}
        for k, (shape, dt) in WSPECS.items():
            W[k] = nc.dram_tensor(k, shape, dt, kind="ExternalInput")
        ident_d = nc.dram_tensor("c_ident", [128, 128], BF16, kind="ExternalInput")
        CONST = {"pos": pos_in.t, "invf64": nc.dram_tensor("c_invf64", [64, 1], F32, kind="ExternalInput"),
                 "invf128": nc.dram_tensor("c_invf128", [128, 1], F32, kind="ExternalInput")}
        CONST["io"] = nc.dram_tensor("c_io", [128, 4, 128], F32, kind="ExternalInput")
        CONST["mask"] = nc.dram_tensor("c_mask", [128, 256], F32, kind="ExternalInput")
        CONST["dij"] = nc.dram_tensor("c_dij", [128, 128], F32, kind="ExternalInput")
        CONST["pcol"] = nc.dram_tensor("c_pcol", [128, 2], F32, kind="ExternalInput")
        SCR = {"oT": C.dram("oT_scr", [SEQ // 128, 128, 8, 128], BF16),
               "rope64": C.dram("rope64_scr", [2, 64, SEQ], F32)}
        kinds = set(k for k, _ in plan)
        if "ret" in kinds:
            SCR_RET = {"y": C.dram("y_ret", [SEQ, 2048], F32), "R": C.dram("R_ret", [SEQ // 128, 128, 8, 512], BF16),
                       "cs": C.dram("cs_ret", [2, 128, SEQ], F32), "V": C.dram("V_ret", [SEQ // 128, 128, 2048], BF16)}
        if "gla" in kinds:
            SCR_GLA = {"y": C.dram("y_gla", [SEQ, 1024], F32), "R": C.dram("R_gla", [SEQ // 128, 128, 4, 256], BF16),
                       "cs": None, "V": C.dram("V_gla", [SEQ // 128, 128, 1024], BF16)}
        out = C.dram("out", [SEQ, D], F32, kind="ExternalOutput")
        xa = C.dram("xa", [SEQ, D], F32)
        xb = C.dram("xb", [SEQ, D], F32)
        ident = C.sb(stack, "ident", [128, 128], BF16)
        S.dma(S.sp, ident[:, :], ident_d[:, :], reads=[], writes=[ident.b], slot=ident.b)
        PS = [C.ps(stack, "ps%d" % i, [128, 512], F32) for i in range(8)]
        src = x_in
        scratch = [xa, xb]
        pre = None
        fst = None
        for i, (kind, li) in enumerate(plan):
            dst = out if i == len(plan) - 1 else scratch[i % 2]
            hook = None
            if kind != "ffn" and i + 1 < len(plan) and plan[i + 1][0] == "ffn":
                which = {"mla": ("w1", "w2"), "ret": ("w2",), "gla": ("w1",)}[kind]
                fst = stack.enter_context(ExitStack())
                box = {}

                def hook(stage, which=which, fli=plan[i + 1][1], fst=fst, box=box):
                    if stage == "alloc":
                        box["pre"], box["new"] = ffn_alloc(C, fst, which)
                    else:
                        ffn_issue(C, W, fli, box["pre"], box["new"])
            if kind == "ffn":
                ffn_sublayer(C, W, li, src, dst, SEQ, PS, ident, pre)
                if fst is not None:
                    C.end(fst)
                    fst.close()
                    fst = None
                pre = None
            elif kind == "mla":
                mla_sublayer(C, W, li, src, dst, SEQ, PS, ident, CONST, SCR, hook)
            elif kind == "ret":
                lin_sublayer(C, W, li, src, dst, SEQ, PS, ident, CONST, SCR_RET, "ret", hook)
            elif kind == "gla":
                lin_sublayer(C, W, li, src, dst, SEQ, PS, ident, CONST, SCR_GLA, "gla", hook)
            else:
                raise NotImplementedError(kind)
            if hook is not None:
                pre = box.get("pre")
            src = dst
        S.barrier()
    return nc


FULL_PLAN = [("mla", 0), ("ffn", 0), ("ret", 1), ("ffn", 1), ("gla", 2), ("ffn", 2), ("mla", 3), ("ffn", 3)]


def run(inputs, SEQ, plan, n_cores, trace=False):
    nc = build_program(SEQ, plan)
    consts = const_inputs()
    in_maps = []
    for c in range(n_cores):
        m = {"x": np.ascontiguousarray(inputs["x"][c]), "positions": np.ascontiguousarray(inputs["positions"][c])}
        for k in WSPECS:
            m[k] = np.ascontiguousarray(inputs[k])
        m.update(consts)
        in_maps.append(m)
    res = run_bass_kernel_spmd(nc, in_maps, core_ids=list(range(n_cores)), trace=trace)
    outs = np.stack([np.asarray(r["out"]) for r in res.results], axis=0)
    return outs, res


def kernel(**inputs):
    inputs = {k: np.asarray(v) for k, v in inputs.items()}
    outs, _ = run(inputs, 4096, FULL_PLAN, 8)
    return outs.astype(np.float32)
```
